# Optimizing a Trainium2 kernel written in Bass

```python
import math
import jax
import jax.numpy as jnp
from jax import lax
import numpy as np

D_MODEL = 2048
BATCH = 4
SEQ = 4096
DEPTH = 4

GRID_W = 64
CTX_LEN = 256
N_MIXERS = 4
D_MIX = D_MODEL
GROUP_W = D_MIX // N_MIXERS
D_FF = 128 * int(round(8 * D_MODEL / 3 / 128))
N_MOD = 9

HY_C = GROUP_W
HY_ORDER = 2
HY_EMB = 33
HY_BANDS = (HY_EMB - 1) // 2
HY_FF = 64
HY_FAST_DECAY = 0.3
HY_SLOW_DECAY = 1.5
HY_TARGET = 1e-2

RW_HEADS = 8
RW_HEAD_DIM = GROUP_W // RW_HEADS
RW_DECAY_LORA = 96
RW_ICLR_LORA = 96
RW_GATE_LORA = 256
RW_GN_EPS = 64e-5

MLA_HEADS = 8
MLA_V = GROUP_W // MLA_HEADS
MLA_NOPE = 64
MLA_ROPE = 32
MLA_Q_RANK = 384
MLA_KV_RANK = 128

SWA_HEADS = 8
SWA_KV_HEADS = 2
SWA_GROUPS = SWA_HEADS // SWA_KV_HEADS
SWA_HEAD_DIM = GROUP_W // SWA_HEADS
SWA_WINDOW = 128
BLOCK = 128

ROPE_BASE = 10000.0
LN_EPS = 1e-5
RMS_EPS = 1e-6
DEEPNORM_ALPHA = (2 * DEPTH) ** 0.25
DEEPNORM_BETA = (8 * DEPTH) ** -0.25
NEG_INF = -1e30

HY_COLS = 3 * HY_C
RW_COLS = 3 * GROUP_W + 2 * RW_DECAY_LORA + 2 * RW_ICLR_LORA + RW_GATE_LORA
MLA_COLS = MLA_Q_RANK + MLA_KV_RANK + MLA_ROPE
SWA_COLS = (SWA_HEADS + 2 * SWA_KV_HEADS) * SWA_HEAD_DIM
IN_COLS = HY_COLS + RW_COLS + MLA_COLS + SWA_COLS

kernel_name = "hybrid_parallel_groups_dit_block"


def layer_norm(x, g, b):
    xf = x.astype(jnp.float32)
    mu = jnp.mean(xf, axis=-1, keepdims=True)
    var = jnp.mean(jnp.square(xf - mu), axis=-1, keepdims=True)
    return ((xf - mu) * lax.rsqrt(var + LN_EPS) * g + b).astype(x.dtype)


def rms_norm(x, g):
    xf = x.astype(jnp.float32)
    return (xf * lax.rsqrt(jnp.mean(jnp.square(xf), axis=-1, keepdims=True) + RMS_EPS) * g).astype(x.dtype)


def modulate(x, mod, slot):
    return x * (1.0 + mod[..., slot + 1, :]) + mod[..., slot, :]


def post_ln_residual(x, y, mod, slot, g, b, weight):
    return layer_norm(DEEPNORM_ALPHA * x + weight * mod[..., slot + 2, :] * y, g, b)


def swiglu(h, w1, w3, w2):
    return (jax.nn.silu(h @ w1) * (h @ w3)) @ w2


def ffn_sublayer(x, mod, slot, g, b, w1, w3, w2):
    return post_ln_residual(x, swiglu(modulate(x, mod, slot), w1, w3, w2), mod, slot, g, b, 0.5)


def short_conv(u, w, b):
    up = jnp.pad(u, ((0, 0), (1, 1), (0, 0)))
    return up[:, :-2] * w[0] + up[:, 1:-1] * w[1] + up[:, 2:] * w[2] + b


def axial_rope_angles(n_tok, rot_dim):
    rows = n_tok // GRID_W
    row = jnp.repeat(jnp.arange(rows), GRID_W)
    col = jnp.tile(jnp.arange(GRID_W), rows)
    n_freq = rot_dim // 4
    inv = ROPE_BASE ** (-jnp.arange(n_freq, dtype=jnp.float32) / n_freq)
    ang = jnp.concatenate([row[:, None] * inv, col[:, None] * inv], axis=-1)
    return jnp.cos(ang), jnp.sin(ang)


def apply_rope(x, cos, sin):
    x1, x2 = jnp.split(x, 2, axis=-1)
    c, s = cos[:, None, :], sin[:, None, :]
    return jnp.concatenate([x1 * c - x2 * s, x2 * c + x1 * s], axis=-1).astype(x.dtype)


def hyena_filters(n_tok, f_w1, f_b1, f_w2, f_b2, f_w3, f_freq):
    t = jnp.linspace(0.0, 1.0, n_tok, dtype=jnp.float32)[:, None]
    w = 2.0 * math.pi * jnp.arange(n_tok, dtype=jnp.float32)[:, None] / n_tok
    f = jnp.linspace(1e-4, HY_BANDS - 1, HY_BANDS, dtype=jnp.float32)[None, :]
    feats = jnp.concatenate([t, jnp.cos(f * w), -jnp.sin(f * w)], axis=-1)
    h = jnp.sin(f_freq[0] * (feats @ f_w1 + f_b1))
    h = jnp.sin(f_freq[1] * (h @ f_w2 + f_b2))
    h = (h @ f_w3).astype(jnp.float32).reshape(n_tok, HY_ORDER, 2, HY_C)
    deltas = jnp.abs(jnp.linspace(math.log(HY_TARGET) / HY_FAST_DECAY, math.log(HY_TARGET) / HY_SLOW_DECAY,
                                  HY_C, dtype=jnp.float32))
    h = h * jnp.exp(-t[:, :, None, None] * deltas)
    fwd, bwd = h[:, :, 0], h[:, :, 1]
    full = jnp.concatenate([fwd, jnp.zeros_like(fwd[:1]), bwd[1:][::-1]], axis=0)
    return full / jnp.sum(jnp.abs(full), axis=0, keepdims=True)


def fft_long_conv(u, h_full, bias):
    n = u.shape[1]
    uf = u.astype(jnp.float32)
    y = jnp.fft.irfft(jnp.fft.rfft(uf, n=2 * n, axis=1) * jnp.fft.rfft(h_full, n=2 * n, axis=0)[None],
                      n=2 * n, axis=1)[:, :n]
    return (y + uf * bias).astype(u.dtype)


def hyena_mixer(z, conv_w, conv_b, f_w1, f_b1, f_w2, f_b2, f_w3, f_freq, bias):
    h_full = hyena_filters(z.shape[1], f_w1, f_b1, f_w2, f_b2, f_w3, f_freq)
    v, x1, x2 = jnp.split(short_conv(z, conv_w, conv_b), 3, axis=-1)
    y = x1 * fft_long_conv(v, h_full[:, 0], bias[0])
    return x2 * fft_long_conv(y, h_full[:, 1], bias[1])


def to_scan(u):
    u = jnp.stack([u[:, :, 0], jnp.flip(u[:, :, 1], axis=1)], axis=2)
    return jnp.transpose(u, (1, 2, 0, 3, 4)).astype(jnp.float32)


def rwkv_prep(z, conv_w, conv_b, w0, w2, a0, a2, k_k, k_a):
    B, T, _ = z.shape
    C, H, N = GROUP_W, RW_HEADS, RW_HEAD_DIM
    r, k, v = jnp.split(short_conv(z[..., :3 * C], conv_w, conv_b), 3, axis=-1)
    o = 3 * C
    wl = z[..., o:o + 2 * RW_DECAY_LORA].reshape(B, T, 2, RW_DECAY_LORA)
    o += 2 * RW_DECAY_LORA
    al = z[..., o:o + 2 * RW_ICLR_LORA].reshape(B, T, 2, RW_ICLR_LORA)
    o += 2 * RW_ICLR_LORA
    gl = z[..., o:]
    w = -jax.nn.softplus(-(w0 + jnp.einsum('btdr,drc->btdc', jnp.tanh(wl), w2))) - 0.5
    decay = jnp.exp(-jnp.exp(w.astype(jnp.float32))).reshape(B, T, 2, H, N)
    a = jax.nn.sigmoid(a0 + jnp.einsum('btdr,drc->btdc', al, a2))
    kk = (k * k_k).reshape(B, T, H, N).astype(jnp.float32)
    kk = kk * lax.rsqrt(jnp.sum(kk * kk, axis=-1, keepdims=True) + 1e-12)
    k_dir = (k[:, :, None] * (1.0 + (a - 1.0) * k_a)).reshape(B, T, 2, H, N)
    a_h = a.reshape(B, T, 2, H, N)
    rh, vh = r.reshape(B, T, H, N), v.reshape(B, T, H, N)
    both = lambda u: jnp.broadcast_to(u[:, :, None], (B, T, 2, H, N))
    scan_in = (to_scan(both(rh)), to_scan(decay), to_scan(k_dir), to_scan(both(vh)),
               to_scan(both(-kk)), to_scan(kk[:, :, None] * a_h))
    return scan_in, rh, vh, k_dir, gl


def rwkv_scan(state, xs, emit):
    def step(S, inp):
        r_t, w_t, k_t, v_t, a_t, b_t = inp
        sa = jnp.einsum('dbhij,dbhj->dbhi', S, a_t)
        S = S * w_t[..., None, :] + sa[..., :, None] * b_t[..., None, :] + v_t[..., :, None] * k_t[..., None, :]
        return S, (jnp.einsum('dbhij,dbhj->dbhi', S, r_t) if emit else None)
    return lax.scan(step, state, xs)


def rwkv_readout(ys, rh, vh, k_dir, gl, g2, r_k, gn_g, gn_b):
    B, T, H, N = rh.shape
    y = jnp.transpose(ys[:, 0] + jnp.flip(ys[:, 1], axis=0), (1, 0, 2, 3))
    mu = jnp.mean(y, axis=-1, keepdims=True)
    var = jnp.mean(jnp.square(y - mu), axis=-1, keepdims=True)
    y = (y - mu) * lax.rsqrt(var + RW_GN_EPS) * gn_g.reshape(H, N) + gn_b.reshape(H, N)
    bonus = jnp.sum(jnp.sum(rh[:, :, None] * k_dir * r_k, axis=-1, keepdims=True), axis=2) * vh
    g = jax.nn.sigmoid(gl) @ g2
    return ((y + bonus).reshape(B, T, H * N) * g).astype(rh.dtype)


def rwkv_mixer(z, zc, conv_w, conv_b, w0, w2, a0, a2, g2, k_k, k_a, r_k, gn_g, gn_b, ctx_out):
    scan_c, rc, vc, kdc, glc = rwkv_prep(zc, conv_w, conv_b, w0, w2, a0, a2, k_k, k_a)
    scan_l, rl, vl, kdl, gll = rwkv_prep(z, conv_w, conv_b, w0, w2, a0, a2, k_k, k_a)
    state0 = jnp.zeros((2, z.shape[0], RW_HEADS, RW_HEAD_DIM, RW_HEAD_DIM), jnp.float32)
    state_c, ys_c = rwkv_scan(state0, scan_c, ctx_out)
    _, ys_l = rwkv_scan(state_c, scan_l, True)
    y = rwkv_readout(ys_l, rl, vl, kdl, gll, g2, r_k, gn_g, gn_b)
    yc = rwkv_readout(ys_c, rc, vc, kdc, glc, g2, r_k, gn_g, gn_b) if ctx_out else None
    return y, yc


def dense_attention_blocks(q, k, v, scale):
    B, T, H, dk = q.shape
    qb = jnp.swapaxes(q.reshape(B, T // BLOCK, BLOCK, H, dk), 0, 1)
    def one_block(q_blk):
        s = jnp.einsum('bqhd,bkhd->bhqk', q_blk, k).astype(jnp.float32) * scale
        p = jax.nn.softmax(s, axis=-1).astype(v.dtype)
        return jnp.einsum('bhqk,bkhd->bqhd', p, v)
    out = lax.map(one_block, qb)
    return jnp.swapaxes(out, 0, 1).reshape(B, T, H * v.shape[-1])


def mla_mixer(z, zc, q_g, wuq, kv_g, wukv, rope, ctx_out):
    def queries(u):
        q = (rms_norm(u[..., :MLA_Q_RANK], q_g) @ wuq).reshape(u.shape[0], u.shape[1], MLA_HEADS, MLA_NOPE + MLA_ROPE)
        return q[..., :MLA_NOPE], q[..., MLA_NOPE:]
    def keys_values(u):
        ckv = rms_norm(u[..., MLA_Q_RANK:MLA_Q_RANK + MLA_KV_RANK], kv_g)
        kv = (ckv @ wukv).reshape(u.shape[0], u.shape[1], MLA_HEADS, MLA_NOPE + MLA_V)
        return kv[..., :MLA_NOPE], kv[..., MLA_NOPE:], u[..., MLA_Q_RANK + MLA_KV_RANK:][:, :, None, :]
    def assemble(nope, pe):
        return jnp.concatenate([nope, jnp.broadcast_to(pe, nope.shape[:3] + (pe.shape[-1],))], axis=-1)
    scale = (MLA_NOPE + MLA_ROPE) ** -0.5
    q_nope, q_pe = queries(z)
    k_nope, v, k_pe = keys_values(z)
    q = assemble(q_nope, apply_rope(q_pe, *rope))
    k = assemble(k_nope, apply_rope(k_pe, *rope))
    kc_nope, vc, kc_pe = keys_values(zc)
    kc = assemble(kc_nope, kc_pe)
    y = dense_attention_blocks(q, jnp.concatenate([k, kc], axis=1), jnp.concatenate([v, vc], axis=1), scale)
    if not ctx_out:
        return y, None
    yc = dense_attention_blocks(assemble(*queries(zc)), kc, vc, scale)
    return y, yc


def softmax_with_sink(scores, sink):
    s_sink = jnp.broadcast_to(sink.astype(jnp.float32).reshape(SWA_KV_HEADS, SWA_GROUPS, 1, 1),
                              scores.shape[:-1] + (1,))
    return jax.nn.softmax(jnp.concatenate([scores, s_sink], axis=-1), axis=-1)[..., :-1]


def swa_latent(q, k, v, kc, vc, sink):
    B, T = q.shape[:2]
    nb = T // BLOCK
    d = SWA_HEAD_DIM
    qb = q.reshape(B, nb, BLOCK, SWA_KV_HEADS, SWA_GROUPS, d)
    def band(u):
        up = jnp.pad(u, ((0, 0), (BLOCK, BLOCK), (0, 0), (0, 0))).reshape(B, nb + 2, BLOCK, SWA_KV_HEADS, d)
        return jnp.concatenate([up[:, :-2], up[:, 1:-1], up[:, 2:]], axis=2)
    kb, vb = band(k), band(v)
    qpos = jnp.arange(nb)[:, None, None] * BLOCK + jnp.arange(BLOCK)[None, :, None]
    kpos = jnp.arange(nb)[:, None, None] * BLOCK - BLOCK + jnp.arange(3 * BLOCK)[None, None, :]
    valid = (jnp.abs(kpos - qpos) <= SWA_WINDOW) & (kpos >= 0) & (kpos < T)
    scale = d ** -0.5
    s_loc = jnp.einsum('bnqhgd,bnkhd->bnhgqk', qb, kb).astype(jnp.float32) * scale
    s_loc = jnp.where(valid[None, :, None, None], s_loc, NEG_INF)
    s_ctx = jnp.einsum('bnqhgd,bkhd->bnhgqk', qb, kc).astype(jnp.float32) * scale
    p = softmax_with_sink(jnp.concatenate([s_loc, s_ctx], axis=-1), sink).astype(v.dtype)
    o = (jnp.einsum('bnhgqk,bnkhd->bnqhgd', p[..., :3 * BLOCK], vb)
         + jnp.einsum('bnhgqk,bkhd->bnqhgd', p[..., 3 * BLOCK:], vc))
    return o.reshape(B, T, SWA_HEADS * d)


def swa_context(qc, kc, vc, sink):
    B, L = qc.shape[:2]
    qg = qc.reshape(B, L, SWA_KV_HEADS, SWA_GROUPS, SWA_HEAD_DIM)
    s = jnp.einsum('bqhgd,bkhd->bhgqk', qg, kc).astype(jnp.float32) * SWA_HEAD_DIM ** -0.5
    p = softmax_with_sink(s, sink).astype(vc.dtype)
    return jnp.einsum('bhgqk,bkhd->bqhgd', p, vc).reshape(B, L, SWA_HEADS * SWA_HEAD_DIM)


def swa_mixer(z, zc, sink, rope, ctx_out):
    nq, nk = SWA_HEADS * SWA_HEAD_DIM, SWA_KV_HEADS * SWA_HEAD_DIM
    def split(u):
        B, T = u.shape[:2]
        return (u[..., :nq].reshape(B, T, SWA_HEADS, SWA_HEAD_DIM),
                u[..., nq:nq + nk].reshape(B, T, SWA_KV_HEADS, SWA_HEAD_DIM),
                u[..., nq + nk:].reshape(B, T, SWA_KV_HEADS, SWA_HEAD_DIM))
    q, k, v = split(z)
    qc, kc, vc = split(zc)
    y = swa_latent(apply_rope(q, *rope), apply_rope(k, *rope), v, kc, vc, sink)
    yc = swa_context(qc, kc, vc, sink) if ctx_out else None
    return y, yc


def token_mixer(h, hc, w_in, w_out, hy, rw, mla, sink, rope_mla, rope_swa, ctx_out):
    z = h @ w_in
    zc = hc @ w_in
    o_rw = HY_COLS
    o_mla = o_rw + RW_COLS
    o_swa = o_mla + MLA_COLS
    y_hy = hyena_mixer(z[..., :o_rw], *hy)
    y_rw, yc_rw = rwkv_mixer(z[..., o_rw:o_mla], zc[..., o_rw:o_mla], *rw, ctx_out)
    y_mla, yc_mla = mla_mixer(z[..., o_mla:o_swa], zc[..., o_mla:o_swa], *mla, rope_mla, ctx_out)
    y_swa, yc_swa = swa_mixer(z[..., o_swa:], zc[..., o_swa:], sink, rope_swa, ctx_out)
    y = jnp.concatenate([y_hy, y_rw, y_mla, y_swa], axis=-1) @ w_out
    if not ctx_out:
        return y, None
    yc = jnp.concatenate([hyena_mixer(zc[..., :o_rw], *hy), yc_rw, yc_mla, yc_swa], axis=-1) @ w_out
    return y, yc


def setup_inputs(seed: int = 0) -> dict:
    key = jax.random.key(seed)
    ks = iter(jax.random.split(key, 48))
    def nrm(shape, scale):
        return jax.random.normal(next(ks), shape, jnp.float32) * scale
    def gain(shape, s=0.05):
        return 1.0 + nrm(shape, s)
    L = DEPTH
    return {
        "x": nrm((BATCH, SEQ, D_MODEL), 1.0),
        "c": nrm((BATCH, D_MODEL), 1.0),
        "ctx": nrm((BATCH, CTX_LEN, D_MODEL), 1.0),
        "c_ctx": nrm((D_MODEL,), 1.0),
        "ada_w": nrm((L, D_MODEL, N_MOD * D_MODEL), 0.5 * D_MODEL ** -0.5),
        "ada_b": nrm((L, N_MOD * D_MODEL), 0.02),
        "ln_g": gain((L, 3, D_MODEL)),
        "ln_b": nrm((L, 3, D_MODEL), 0.02),
        "ffn_w1": nrm((L, 2, D_MODEL, D_FF), D_MODEL ** -0.5),
        "ffn_w3": nrm((L, 2, D_MODEL, D_FF), D_MODEL ** -0.5),
        "ffn_w2": nrm((L, 2, D_FF, D_MODEL), DEEPNORM_BETA * D_FF ** -0.5),
        "w_in": nrm((L, D_MODEL, IN_COLS), D_MODEL ** -0.5),
        "w_out": nrm((L, D_MIX, D_MODEL), DEEPNORM_BETA * D_MIX ** -0.5),
        "hy_conv_w": nrm((L, 3, HY_COLS), 3 ** -0.5),
        "hy_conv_b": nrm((L, HY_COLS), 0.02),
        "hy_f_w1": nrm((L, HY_EMB, HY_FF), HY_EMB ** -0.5),
        "hy_f_b1": nrm((L, HY_FF), 0.1),
        "hy_f_w2": nrm((L, HY_FF, HY_FF), HY_FF ** -0.5),
        "hy_f_b2": nrm((L, HY_FF), 0.1),
        "hy_f_w3": nrm((L, HY_FF, HY_ORDER * 2 * HY_C), HY_FF ** -0.5),
        "hy_f_freq": gain((L, 2, HY_FF), 0.1),
        "hy_bias": nrm((L, HY_ORDER, HY_C), 0.5),
        "rw_conv_w": nrm((L, 3, 3 * GROUP_W), 3 ** -0.5),
        "rw_conv_b": nrm((L, 3 * GROUP_W), 0.02),
        "rw_w0": jax.random.uniform(next(ks), (L, 2, GROUP_W), jnp.float32, minval=-6.0, maxval=-1.0),
        "rw_w2": nrm((L, 2, RW_DECAY_LORA, GROUP_W), RW_DECAY_LORA ** -0.5),
        "rw_a0": nrm((L, 2, GROUP_W), 0.1),
        "rw_a2": nrm((L, 2, RW_ICLR_LORA, GROUP_W), RW_ICLR_LORA ** -0.5),
        "rw_g2": nrm((L, RW_GATE_LORA, GROUP_W), RW_GATE_LORA ** -0.5),
        "rw_k_k": 0.85 + nrm((L, GROUP_W), 0.05),
        "rw_k_a": gain((L, GROUP_W)),
        "rw_r_k": nrm((L, RW_HEADS, RW_HEAD_DIM), 0.1),
        "rw_gn_g": gain((L, GROUP_W)),
        "rw_gn_b": nrm((L, GROUP_W), 0.02),
        "mla_q_g": gain((L, MLA_Q_RANK)),
        "mla_wuq": nrm((L, MLA_Q_RANK, MLA_HEADS * (MLA_NOPE + MLA_ROPE)), MLA_Q_RANK ** -0.5),
        "mla_kv_g": gain((L, MLA_KV_RANK)),
        "mla_wukv": nrm((L, MLA_KV_RANK, MLA_HEADS * (MLA_NOPE + MLA_V)), MLA_KV_RANK ** -0.5),
        "swa_sink": nrm((L, SWA_HEADS), 0.5),
    }


def reference(x, c, ctx, c_ctx, ada_w, ada_b, ln_g, ln_b, ffn_w1, ffn_w3, ffn_w2, w_in, w_out,
              hy_conv_w, hy_conv_b, hy_f_w1, hy_f_b1, hy_f_w2, hy_f_b2, hy_f_w3, hy_f_freq, hy_bias,
              rw_conv_w, rw_conv_b, rw_w0, rw_w2, rw_a0, rw_a2, rw_g2, rw_k_k, rw_k_a, rw_r_k, rw_gn_g, rw_gn_b,
              mla_q_g, mla_wuq, mla_kv_g, mla_wukv, swa_sink):
    n_tok = x.shape[1]
    rope_mla = axial_rope_angles(n_tok, MLA_ROPE)
    rope_swa = axial_rope_angles(n_tok, SWA_HEAD_DIM)
    xc = ctx
    for l in range(DEPTH):
        ctx_out = l < DEPTH - 1
        mod = (jax.nn.silu(c) @ ada_w[l] + ada_b[l]).reshape(c.shape[0], 1, N_MOD, D_MODEL)
        modc = (jax.nn.silu(c_ctx) @ ada_w[l] + ada_b[l]).reshape(N_MOD, D_MODEL)
        x = ffn_sublayer(x, mod, 0, ln_g[l, 0], ln_b[l, 0], ffn_w1[l, 0], ffn_w3[l, 0], ffn_w2[l, 0])
        xc = ffn_sublayer(xc, modc, 0, ln_g[l, 0], ln_b[l, 0], ffn_w1[l, 0], ffn_w3[l, 0], ffn_w2[l, 0])
        y, yc = token_mixer(
            modulate(x, mod, 3), modulate(xc, modc, 3), w_in[l], w_out[l],
            (hy_conv_w[l], hy_conv_b[l], hy_f_w1[l], hy_f_b1[l], hy_f_w2[l], hy_f_b2[l], hy_f_w3[l],
             hy_f_freq[l], hy_bias[l]),
            (rw_conv_w[l], rw_conv_b[l], rw_w0[l], rw_w2[l], rw_a0[l], rw_a2[l], rw_g2[l], rw_k_k[l],
             rw_k_a[l], rw_r_k[l], rw_gn_g[l], rw_gn_b[l]),
            (mla_q_g[l], mla_wuq[l], mla_kv_g[l], mla_wukv[l]),
            swa_sink[l], rope_mla, rope_swa, ctx_out)
        x = post_ln_residual(x, y, mod, 3, ln_g[l, 1], ln_b[l, 1], 1.0)
        x = ffn_sublayer(x, mod, 6, ln_g[l, 2], ln_b[l, 2], ffn_w1[l, 1], ffn_w3[l, 1], ffn_w2[l, 1])
        if ctx_out:
            xc = post_ln_residual(xc, yc, modc, 3, ln_g[l, 1], ln_b[l, 1], 1.0)
            xc = ffn_sublayer(xc, modc, 6, ln_g[l, 2], ln_b[l, 2], ffn_w1[l, 1], ffn_w3[l, 1], ffn_w2[l, 1])
    return x
```

```python
import numpy as np
import math
import concourse.bass as bass
import concourse.mybir as mybir
from concourse.bass_utils import run_bass_kernel_spmd

F32 = mybir.dt.float32
BF16 = mybir.dt.bfloat16
AF = mybir.ActivationFunctionType
ALU = mybir.AluOpType
AX = mybir.AxisListType

N_DMA_SLOTS = 12
STRICT = True


class Buf:
    __slots__ = ("name", "lw", "rd")

    def __init__(self, name=""):
        self.name = name
        self.lw = []
        self.rd = {}


class KB:
    ENGS = ("pe", "act", "dve", "pool", "sp")

    def __init__(self, nc):
        self.nc = nc
        self.lists = {e: [] for e in self.ENGS}
        self.cnt = {e: 0 for e in self.ENGS}
        self.waited = {e: {} for e in self.ENGS}
        self.dma_cnt = {}
        self.dma_rr = {e: 0 for e in self.ENGS}
        self.sems = {}
        self.semkeys = []
        for e in ("pe", "act", "dve", "pool"):
            self.semkeys.append(e)
        for q in ("sp", "act", "pool"):
            for s in range(N_DMA_SLOTS):
                k = "d_%s_%d" % (q, s)
                self.semkeys.append(k)
                self.dma_cnt[k] = 0
        self.n_instr = 0
        import contextlib
        self.stack = contextlib.ExitStack()

    def sbuf(self, name, shape, dtype):
        return self.stack.enter_context(self.nc.sbuf_tensor(name, list(shape), dtype))

    def psum(self, name, shape, dtype):
        return self.stack.enter_context(self.nc.psum_tensor(name, list(shape), dtype))

    def _need(self, eng, deps, is_dma=False):
        w = self.waited[eng]
        out = []
        best = {}
        for d in deps:
            if d is None:
                continue
            k, v = d
            if k == eng and not is_dma and (eng == 'pe' or not STRICT):
                continue
            if w.get(k, 0) >= v:
                continue
            if best.get(k, 0) < v:
                best[k] = v
        for k, v in best.items():
            w[k] = v
            out.append((k, v))
        return out

    def _deps(self, reads, writes, accum=False):
        deps = []
        for b in reads:
            deps.extend(b.lw)
        for b in writes:
            if not accum:
                deps.extend(b.lw)
            deps.extend(b.rd.items())
        return deps

    def _mark(self, tok, reads, writes, accum=False):
        for b in reads:
            if b.rd.get(tok[0], 0) < tok[1]:
                b.rd[tok[0]] = tok[1]
        for b in writes:
            if accum:
                b.lw = b.lw + [tok]
            else:
                b.lw = [tok]
            b.rd = {}

    def op(self, eng, fn, reads=(), writes=(), accum=False):
        waits = self._need(eng, self._deps(reads, writes, accum))
        self.cnt[eng] += 1
        idx = self.cnt[eng]
        self.lists[eng].append((waits, fn, (eng, 1)))
        tok = (eng, idx)
        self._mark(tok, reads, writes, accum)
        self.n_instr += 1 + len(waits)
        return tok

    def dma(self, q, fn, reads=(), writes=(), accum=False):
        s = self.dma_rr[q]
        self.dma_rr[q] = (s + 1) % N_DMA_SLOTS
        k = "d_%s_%d" % (q, s)
        deps = self._deps(reads, writes, accum)
        if self.dma_cnt[k] > 0:
            deps.append((k, self.dma_cnt[k]))
        waits = self._need(q, deps, is_dma=True)
        self.dma_cnt[k] += 16
        tok = (k, self.dma_cnt[k])
        self.lists[q].append((waits, fn, (k, 16)))
        self._mark(tok, reads, writes, accum)
        self.n_instr += 1 + len(waits)
        return tok

    def finish(self, out_bufs):
        deps = []
        for b in out_bufs:
            deps.extend(b.lw)
        waits = self._need("sp", deps)
        self.lists["sp"].append((waits, None, None))

    def emit(self):
        nc = self.nc
        with self.stack as st:
            for k in self.semkeys:
                self.sems[k] = st.enter_context(nc.semaphore(k))
            block = st.enter_context(nc.Block())
            sems = self.sems

            def run(e, lst):
                for waits, fn, inc in lst:
                    for (k, v) in waits:
                        e.wait_ge(sems[k], v)
                    if fn is not None:
                        ins = fn(e)
                        ins.then_inc(sems[inc[0]], inc[1])

            @block.tensor
            def _(e):
                run(e, self.lists["pe"])

            @block.scalar
            def _(e):
                run(e, self.lists["act"])

            @block.vector
            def _(e):
                run(e, self.lists["dve"])

            @block.gpsimd
            def _(e):
                run(e, self.lists["pool"])

            @block.sync
            def _(e):
                run(e, self.lists["sp"])


D = 2048
DFF = 5504
KC = 16
FC = 43
INC = 5024
T_A = 2176
NT_A = 17
ALPHA = 8.0 ** 0.25
LN_EPS = 1e-5
GROUPS_A = [(0, 4), (4, 8), (8, 12), (12, 16), (16, 17)]


def build_A(has_prev, has_cur):
    nc = bass.Bass("TRN2", target_bir_lowering=False)
    kb = KB(nc)

    def din(name, shape, dt=F32):
        return nc.dram_tensor(name, list(shape), dt, kind="ExternalInput").ap()

    def dout(name, shape, dt=F32):
        return nc.dram_tensor(name, list(shape), dt, kind="ExternalOutput").ap()

    def dscr(name, shape, dt=F32):
        return nc.dram_tensor(name, list(shape), dt, kind="Internal").ap()

    xin = din("xin", [T_A, D])
    ident_d = din("ident", [128, 128])
    if has_prev:
        ycat = din("ycat", [T_A, D])
        modprev = din("modprev", [2, 9 * D])
        wout = din("wout", [D, D])
        pw1 = din("pw1", [D, DFF]); pw3 = din("pw3", [D, DFF]); pw2 = din("pw2", [DFF, D])
        plng = din("plng", [3, D]); plnb = din("plnb", [3, D])
        xa = dscr("xa", [T_A, D])
        xb = dscr("xb", [T_A, D]) if has_cur else dout("xfin", [T_A, D])
    if has_cur:
        cvec = din("cvec", [2, D])
        adaw = din("adaw", [D, 9 * D]); adab = din("adab", [1, 9 * D])
        w1 = din("w1", [D, DFF]); w3 = din("w3", [D, DFF]); w2 = din("w2", [DFF, D])
        win = din("win", [D, INC])
        lng = din("lng", [3, D]); lnb = din("lnb", [3, D])
        x1 = dout("x1", [T_A, D]); z = dout("z", [T_A, INC]); modout = dout("modout", [2, 9 * D])
    f = dscr("fscr", [T_A, D])

    def tb(n=NT_A):
        return [Buf() for _ in range(n)]
    b_f = tb(); b_xa = tb(); b_xb = tb(); b_x1 = tb(); b_z = tb()
    b_mod = Buf()

    ident = kb.sbuf("ident_sb", [128, 128], F32); b_ident = Buf()
    xt = [kb.sbuf("xt%d" % i, [128, D], F32) for i in range(2)]; b_xt = [Buf(), Buf()]
    ft = [kb.sbuf("ft%d" % i, [128, D], F32) for i in range(2)]; b_ft = [Buf(), Buf()]
    hT = kb.sbuf("hT", [128, KC, 512], BF16); b_hT = [Buf() for _ in range(4)]
    uT = kb.sbuf("uT", [128, FC, 512], BF16); b_uT = [Buf() for _ in range(FC)]
    wb = [kb.sbuf("wb%d" % i, [128, 8192], BF16) for i in range(4)]; b_wb = [Buf() for _ in range(4)]
    V = {n: kb.sbuf("v_" + n, [128, D], F32) for n in ("gate", "g", "b", "shift", "sc1p")}
    b_V = {n: Buf() for n in V}
    st = [kb.sbuf("st%d" % i, [128, 512], F32) for i in range(2)]; b_st = [Buf(), Buf()]
    su = [kb.sbuf("su%d" % i, [128, 512], F32) for i in range(2)]; b_su = [Buf(), Buf()]
    stats = kb.sbuf("stats", [128, 4, 6], F32); b_stats = Buf()
    mv = kb.sbuf("mv", [128, 4], F32); b_mv = Buf()
    pT = [kb.psum("pT%d" % i, [128, 4, 128], F32) for i in range(2)]; b_pT = [Buf(), Buf()]
    pA = [kb.psum("pA%d" % i, [128, 512], F32) for i in range(2)]; b_pA = [Buf(), Buf()]
    pB = [kb.psum("pB%d" % i, [128, 512], F32) for i in range(2)]; b_pB = [Buf(), Buf()]
    pC = [kb.psum("pC%d" % i, [128, 512], F32) for i in range(2)]; b_pC = [Buf(), Buf()]
    rr = {"xt": 0, "ft": 0, "wb": 0, "st": 0, "su": 0, "pT": 0, "pA": 0, "pC": 0, "ev": 0}

    def nxt(k, n):
        v = rr[k]; rr[k] = (v + 1) % n
        return v

    kb.dma("sp", lambda e: e.dma_start(out=ident[:], in_=ident_d), [], [b_ident])

    def load_vec(name, src_row_ap, post=None):
        kb.dma("sp", lambda e: e.dma_start(out=V[name][:], in_=src_row_ap.partition_broadcast(128)), [b_mod], [b_V[name]])
        if post is not None:
            kb.op("dve", lambda e: post(e, V[name]), [b_V[name]], [b_V[name]])

    def mrow(modt, r, slot):
        return modt[r:r + 1, slot * D:(slot + 1) * D]

    def load_modulate(modt, r, slot):
        load_vec("shift", mrow(modt, r, slot))
        load_vec("sc1p", mrow(modt, r, slot + 1), lambda e, t: e.tensor_scalar_add(out=t[:], in0=t[:], scalar1=1.0))

    def load_ln(modt, r, slot, wgt, g_t, b_t, idx):
        load_vec("gate", mrow(modt, r, slot + 2),
                 (lambda e, t: e.tensor_scalar_mul(out=t[:], in0=t[:], scalar1=float(wgt))) if wgt != 1.0 else None)
        load_vec("g", g_t[idx:idx + 1, :])
        load_vec("b", b_t[idx:idx + 1, :])

    def transpose_in(src_t, b_src, j):
        for q in range(4):
            p = nxt("pT", 2)
            for i in range(4):
                kc = q * 4 + i
                kb.op("pe", lambda e, p=p, i=i, kc=kc: e.transpose(out=pT[p][:, i, :], in_=src_t[:, kc * 128:(kc + 1) * 128],
                                                                     identity=ident[:]), [b_src, b_ident], [b_pT[p]])
            eng = "act" if nxt("ev", 2) == 0 else "dve"
            if eng == "act":
                kb.op("act", lambda e, p=p, q=q: e.copy(out=hT[:, q * 4:(q + 1) * 4, j * 128:(j + 1) * 128], in_=pT[p][:]),
                      [b_pT[p]], [b_hT[j]])
            else:
                kb.op("dve", lambda e, p=p, q=q: e.tensor_copy(out=hT[:, q * 4:(q + 1) * 4, j * 128:(j + 1) * 128], in_=pT[p][:]),
                      [b_pT[p]], [b_hT[j]])

    def rows(t, tt):
        return t[tt * 128:(tt + 1) * 128, :]

    def modulate_to_hT(src_t, b_src, tmp_t, b_tmp, j):
        kb.op("dve", lambda e: e.tensor_tensor(out=tmp_t[:], in0=src_t[:], in1=V["sc1p"][:], op=ALU.mult), [b_src, b_V["sc1p"]], [b_tmp])
        kb.op("dve", lambda e: e.tensor_tensor(out=tmp_t[:], in0=tmp_t[:], in1=V["shift"][:], op=ALU.add), [b_tmp, b_V["shift"]], [b_tmp])
        transpose_in(tmp_t, b_tmp, j)

    def rowop_mod(g, xsrc, b_xsrc):
        t0, t1 = g
        for tt in range(t0, t1):
            xi = nxt("xt", 2); fi = nxt("ft", 2)
            kb.dma("sp", lambda e, tt=tt, xi=xi: e.dma_start(out=xt[xi][:], in_=rows(xsrc, tt)), [b_xsrc[tt]] if b_xsrc else [], [b_xt[xi]])
            modulate_to_hT(xt[xi], b_xt[xi], ft[fi], b_ft[fi], tt - t0)

    def rowop_plain(g, src, b_src):
        t0, t1 = g
        for tt in range(t0, t1):
            xi = nxt("xt", 2)
            kb.dma("sp", lambda e, tt=tt, xi=xi: e.dma_start(out=xt[xi][:], in_=rows(src, tt)), [b_src[tt]] if b_src else [], [b_xt[xi]])
            transpose_in(xt[xi], b_xt[xi], tt - t0)

    def rowop_ln(g, xsrc, b_xsrc, dst, b_dst, do_mod):
        t0, t1 = g
        for tt in range(t0, t1):
            xi = nxt("xt", 2); fi = nxt("ft", 2)
            X = xt[xi]; Fb = ft[fi]
            kb.dma("sp", lambda e, tt=tt, X=X: e.dma_start(out=X[:], in_=rows(xsrc, tt)), [b_xsrc[tt]] if b_xsrc else [], [b_xt[xi]])
            kb.dma("sp", lambda e, tt=tt, Fb=Fb: e.dma_start(out=Fb[:], in_=rows(f, tt)), [b_f[tt]], [b_ft[fi]])
            kb.op("dve", lambda e, Fb=Fb: e.tensor_tensor(out=Fb[:], in0=Fb[:], in1=V["gate"][:], op=ALU.mult), [b_ft[fi], b_V["gate"]], [b_ft[fi]])
            kb.op("dve", lambda e, X=X, Fb=Fb: e.scalar_tensor_tensor(out=X[:], in0=X[:], scalar=float(ALPHA), in1=Fb[:], op0=ALU.mult, op1=ALU.add),
                  [b_xt[xi], b_ft[fi]], [b_xt[xi]])
            for c in range(4):
                kb.op("dve", lambda e, X=X, c=c: e.bn_stats(out=stats[:, c, :], in_=X[:, c * 512:(c + 1) * 512]), [b_xt[xi]], [b_stats])
            kb.op("dve", lambda e: e.bn_aggr(out=mv[:, 0:2], in_=stats[:].rearrange("p a b -> p (a b)")), [b_stats], [b_mv])
            kb.op("act", lambda e: e.activation(out=mv[:, 2:3], in_=mv[:, 1:2], func=AF.Sqrt, bias=float(LN_EPS), scale=1.0), [b_mv], [b_mv])
            kb.op("dve", lambda e: e.reciprocal(out=mv[:, 3:4], in_=mv[:, 2:3]), [b_mv], [b_mv])
            kb.op("dve", lambda e, X=X: e.tensor_scalar(out=X[:], in0=X[:], scalar1=mv[:, 0:1], scalar2=mv[:, 3:4], op0=ALU.subtract, op1=ALU.mult),
                  [b_xt[xi], b_mv], [b_xt[xi]])
            kb.op("dve", lambda e, X=X: e.tensor_tensor(out=X[:], in0=X[:], in1=V["g"][:], op=ALU.mult), [b_xt[xi], b_V["g"]], [b_xt[xi]])
            kb.op("dve", lambda e, X=X: e.tensor_tensor(out=X[:], in0=X[:], in1=V["b"][:], op=ALU.add), [b_xt[xi], b_V["b"]], [b_xt[xi]])
            kb.dma("sp", lambda e, tt=tt, X=X: e.dma_start(out=rows(dst, tt), in_=X[:]), [b_xt[xi]], [b_dst[tt]])
            if do_mod:
                modulate_to_hT(X, b_xt[xi], Fb, b_ft[fi], tt - t0)

    def wload(slot, src_ap, kc_n, cw):
        view = wb[slot][:, 0:kc_n * cw].rearrange("p (c n) -> p c n", n=cw)
        srcv = src_ap.rearrange("(c p) n -> p c n", p=128)
        step = 4 if cw > 256 else 8
        for k0 in range(0, kc_n, step):
            k1 = min(kc_n, k0 + step)
            kb.dma("pool", lambda e, k0=k0, k1=k1: e.dma_start(out=view[:, k0:k1, :], in_=srcv[:, k0:k1, :]), [], [b_wb[slot]], accum=(k0 > 0))
        return view

    def evac(psrc, b_psrc, dst_dram_ap, b_dst, rows_n, cw, accum=False):
        si = nxt("st", 2)
        eng = "act" if nxt("ev", 2) == 0 else "dve"
        if eng == "act":
            kb.op("act", lambda e: e.copy(out=st[si][0:rows_n, 0:cw], in_=psrc), [b_psrc], [b_st[si]])
        else:
            kb.op("dve", lambda e: e.tensor_copy(out=st[si][0:rows_n, 0:cw], in_=psrc), [b_psrc], [b_st[si]])
        kb.dma("sp", lambda e: e.dma_start(out=dst_dram_ap, in_=st[si][0:rows_n, 0:cw]), [b_st[si]], [b_dst], accum=accum)

    def gemm_T(g, W, N, dst, b_dst):
        t0, t1 = g
        nblk = (N + 511) // 512
        for nb in range(nblk):
            c0 = nb * 512; cw = min(512, N - c0)
            s = nxt("wb", 4)
            wv = wload(s, W[:, c0:c0 + cw], KC, cw)
            for tt in range(t0, t1):
                j = tt - t0
                p = nxt("pC", 2)
                for kc in range(KC):
                    kb.op("pe", lambda e, p=p, kc=kc, j=j, wv=wv, cw=cw: e.matmul(pC[p][:, 0:cw], lhsT=hT[:, kc, j * 128:(j + 1) * 128], rhs=wv[:, kc, :],
                                                                                    start=(kc == 0), stop=(kc == KC - 1)),
                          [b_hT[j], b_wb[s]], [b_pC[p]])
                evac(pC[p][:, 0:cw], b_pC[p], dst[tt * 128:(tt + 1) * 128, c0:c0 + cw], b_dst[tt], 128, cw, accum=(nb > 0))

    def gemm1(g, W1, W3):
        t0, t1 = g
        ntok = (t1 - t0) * 128
        rb = [b_hT[j] for j in range(t1 - t0)]
        for cb in range(11):
            c0 = cb * 512; cw = min(512, DFF - c0)
            s1 = nxt("wb", 4); v1 = wload(s1, W1[:, c0:c0 + cw], KC, cw)
            s3 = nxt("wb", 4); v3 = wload(s3, W3[:, c0:c0 + cw], KC, cw)
            for jj in range(cw // 128):
                fc = cb * 4 + jj
                p = nxt("pA", 2)
                for kc in range(KC):
                    kb.op("pe", lambda e, p=p, kc=kc, jj=jj, v1=v1: e.matmul(pA[p][:, 0:ntok], lhsT=v1[:, kc, jj * 128:(jj + 1) * 128], rhs=hT[:, kc, 0:ntok],
                                                                           start=(kc == 0), stop=(kc == KC - 1)), rb + [b_wb[s1]], [b_pA[p]])
                for kc in range(KC):
                    kb.op("pe", lambda e, p=p, kc=kc, jj=jj, v3=v3: e.matmul(pB[p][:, 0:ntok], lhsT=v3[:, kc, jj * 128:(jj + 1) * 128], rhs=hT[:, kc, 0:ntok],
                                                                           start=(kc == 0), stop=(kc == KC - 1)), rb + [b_wb[s3]], [b_pB[p]])
                si = nxt("su", 2)
                kb.op("act", lambda e, p=p, si=si: e.activation(out=su[si][:, 0:ntok], in_=pA[p][:, 0:ntok], func=AF.Silu), [b_pA[p]], [b_su[si]])
                kb.op("dve", lambda e, p=p, si=si, fc=fc: e.tensor_tensor(out=uT[:, fc, 0:ntok], in0=su[si][:, 0:ntok], in1=pB[p][:, 0:ntok], op=ALU.mult),
                      [b_su[si], b_pB[p]], [b_uT[fc]])

    def gemm2(g, W2):
        t0, t1 = g
        for nb in range(8):
            c0 = nb * 256
            sA = nxt("wb", 4); vA = wload(sA, W2[0:22 * 128, c0:c0 + 256], 22, 256)
            sB = nxt("wb", 4); vB = wload(sB, W2[22 * 128:DFF, c0:c0 + 256], 21, 256)
            for tt in range(t0, t1):
                j = tt - t0
                p = nxt("pC", 2)
                for kc in range(FC):
                    wv, s, k2 = (vA, sA, kc) if kc < 22 else (vB, sB, kc - 22)
                    kb.op("pe", lambda e, p=p, kc=kc, j=j, wv=wv, k2=k2: e.matmul(pC[p][:, 0:256], lhsT=uT[:, kc, j * 128:(j + 1) * 128], rhs=wv[:, k2, :],
                                                                                    start=(kc == 0), stop=(kc == FC - 1)),
                          [b_uT[kc], b_wb[s]], [b_pC[p]])
                evac(pC[p][:, 0:256], b_pC[p], f[tt * 128:(tt + 1) * 128, c0:c0 + 256], b_f[tt], 128, 256, accum=(nb > 0))

    if has_cur:
        cs = ft[0][0:2, :]; b_cs = b_ft[0]
        cT = kb.sbuf("cT", [128, KC, 2], BF16); b_cT = Buf()
        ab = kb.sbuf("ab", [2, 512], F32); b_ab = Buf()
        kb.dma("sp", lambda e: e.dma_start(out=cs, in_=cvec), [], [b_cs])
        kb.op("act", lambda e: e.activation(out=cs, in_=cs, func=AF.Silu), [b_cs], [b_cs])
        for q in range(4):
            p = nxt("pT", 2)
            for i in range(4):
                kc = q * 4 + i
                kb.op("pe", lambda e, p=p, i=i, kc=kc: e.transpose(out=pT[p][:, i, 0:2], in_=ft[0][0:2, kc * 128:(kc + 1) * 128], identity=ident[0:2, 0:2]),
                      [b_cs, b_ident], [b_pT[p]])
            kb.op("dve", lambda e, p=p, q=q: e.tensor_copy(out=cT[:, q * 4:(q + 1) * 4, :], in_=pT[p][:, :, 0:2]), [b_pT[p]], [b_cT])
        for nb in range(9 * D // 512):
            c0 = nb * 512
            s = nxt("wb", 4)
            wv = wload(s, adaw[:, c0:c0 + 512], KC, 512)
            p = nxt("pC", 2)
            for kc in range(KC):
                kb.op("pe", lambda e, p=p, kc=kc, wv=wv: e.matmul(pC[p][0:2, :], lhsT=cT[:, kc, :], rhs=wv[:, kc, :], start=(kc == 0), stop=(kc == KC - 1)),
                      [b_cT, b_wb[s]], [b_pC[p]])
            kb.dma("sp", lambda e, c0=c0: e.dma_start(out=ab[:], in_=adab[0:1, c0:c0 + 512].partition_broadcast(2)), [], [b_ab])
            kb.op("dve", lambda e, p=p: e.tensor_tensor(out=ab[:], in0=ab[:], in1=pC[p][0:2, :], op=ALU.add), [b_ab, b_pC[p]], [b_ab])
            kb.dma("sp", lambda e, c0=c0: e.dma_start(out=modout[:, c0:c0 + 512], in_=ab[:]), [b_ab], [b_mod], accum=(nb > 0))

    outs = []
    for gi, g in enumerate(GROUPS_A):
        r = 0 if gi < 4 else 1
        if has_prev:
            rowop_plain(g, ycat, None)
            gemm_T(g, wout, D, f, b_f)
            load_ln(modprev, r, 3, 1.0, plng, plnb, 1)
            load_modulate(modprev, r, 6)
            rowop_ln(g, xin, None, xa, b_xa, True)
            gemm1(g, pw1, pw3); gemm2(g, pw2)
            load_ln(modprev, r, 6, 0.5, plng, plnb, 2)
            if has_cur:
                load_modulate(modout, r, 0)
            rowop_ln(g, xa, b_xa, xb, b_xb, has_cur)
            cur, b_cur = xb, b_xb
        else:
            load_modulate(modout, r, 0)
            rowop_mod(g, xin, None)
            cur, b_cur = xin, None
        if has_cur:
            gemm1(g, w1, w3); gemm2(g, w2)
            load_ln(modout, r, 0, 0.5, lng, lnb, 0)
            load_modulate(modout, r, 3)
            rowop_ln(g, cur, b_cur, x1, b_x1, True)
            gemm_T(g, win, INC, z, b_z)
    if has_cur:
        outs = b_x1 + b_z + [b_mod]
    else:
        outs = b_xb
    kb.finish(outs)
    kb.emit()
    return nc


T_B = 4352
NT_B = 34


class Ctx:
    def __init__(self):
        self.nc = bass.Bass("TRN2", target_bir_lowering=False)
        self.kb = KB(self.nc)
        self.rr = {}

    def din(self, name, shape, dt=F32):
        return self.nc.dram_tensor(name, list(shape), dt, kind="ExternalInput").ap()

    def dout(self, name, shape, dt=F32):
        return self.nc.dram_tensor(name, list(shape), dt, kind="ExternalOutput").ap()

    def dscr(self, name, shape, dt=F32):
        return self.nc.dram_tensor(name, list(shape), dt, kind="Internal").ap()

    def nxt(self, k, n):
        v = self.rr.get(k, 0); self.rr[k] = (v + 1) % n
        return v

    def sb(self, name, shape, dt=F32, n=1, aslist=False):
        ts = [self.kb.sbuf("%s%d" % (name, i), shape, dt) for i in range(n)]
        bs = [Buf() for _ in range(n)]
        return (ts, bs) if (n > 1 or aslist) else (ts[0], bs[0])

    def ps(self, name, shape, dt=F32, n=1, aslist=False):
        ts = [self.kb.psum("%s%d" % (name, i), shape, dt) for i in range(n)]
        bs = [Buf() for _ in range(n)]
        return (ts, bs) if (n > 1 or aslist) else (ts[0], bs[0])


def attention_core(cx, QT, b_QT, KT, b_KT, Vp, b_Vp, dk, jobs, ybuf, b_ybuf, h, extra_den=None, b_extra=None):
    kb = cx.kb
    pS, b_pS = cx.pS, cx.b_pS
    pO, b_pO = cx.pO, cx.b_pO
    PT, b_PT = cx.PT, cx.b_PT
    for (q0, nq, keys) in jobs:
        nsub = nq // 128
        o = cx.nxt("pO", 2)
        for ki, (kt, mask, b_mask) in enumerate(keys):
            s = cx.nxt("pS", 2)
            kb.op("pe", lambda e, s=s, kt=kt, q0=q0, nq=nq: e.matmul(pS[s][:, 0:nq], lhsT=KT[0:dk, kt * 128:(kt + 1) * 128], rhs=QT[0:dk, q0:q0 + nq], start=True, stop=True),
                  [b_KT, b_QT], [b_pS[s]])
            t = cx.nxt("PT", 2)
            kb.op("act", lambda e, s=s, t=t, nq=nq: e.activation(out=PT[t][:, 0:nq], in_=pS[s][:, 0:nq], func=AF.Exp), [b_pS[s]], [b_PT[t]])
            if mask is not None:
                kb.op("dve", lambda e, t=t, mask=mask, nq=nq: e.tensor_tensor(out=PT[t][:, 0:nq], in0=PT[t][:, 0:nq], in1=mask, op=ALU.mult), [b_PT[t], b_mask], [b_PT[t]])
            for sub in range(nsub):
                kb.op("pe", lambda e, o=o, t=t, sub=sub, kt=kt, ki=ki, nk=len(keys): e.matmul(pO[o][:, sub, 0:65], lhsT=PT[t][:, sub * 128:(sub + 1) * 128], rhs=Vp[:, kt, :],
                                                                                 start=(ki == 0), stop=(ki == nk - 1)),
                      [b_PT[t], b_Vp], [b_pO[o]])
        for sub in range(nsub):
            qt = (q0 // 128) + sub
            den, b_den = cx.den, cx.b_den
            if extra_den is not None:
                kb.op("dve", lambda e, o=o, sub=sub: e.tensor_tensor(out=den[:, 0:1], in0=pO[o][:, sub, 64:65], in1=extra_den, op=ALU.add), [b_pO[o], b_extra], [b_den])
                kb.op("dve", lambda e: e.reciprocal(out=den[:, 1:2], in_=den[:, 0:1]), [b_den], [b_den])
            else:
                kb.op("dve", lambda e, o=o, sub=sub: e.reciprocal(out=den[:, 1:2], in_=pO[o][:, sub, 64:65]), [b_pO[o]], [b_den])
            kb.op("dve", lambda e, o=o, sub=sub, qt=qt: e.tensor_scalar(out=ybuf[qt][:, h * 64:(h + 1) * 64], in0=pO[o][:, sub, 0:64], scalar1=den[:, 1:2], scalar2=None, op0=ALU.mult),
                  [b_pO[o], b_den], [b_ybuf[qt]], accum=True)


def attn_common_alloc(cx, dk):
    cx.pS, cx.b_pS = cx.ps("pS", [128, 512], F32, 2)
    cx.pO, cx.b_pO = cx.ps("pO", [128, 4, 128], F32, 2)
    cx.PT, cx.b_PT = cx.sb("PT", [128, 512], BF16, 2)
    cx.den, cx.b_den = cx.sb("den", [128, 2], F32)
    cx.pT, cx.b_pT = cx.ps("pT", [128, 4, 128], BF16, 1, aslist=True)
    cx.pTf, cx.b_pTf = cx.ps("pTf", [128, 4, 128], F32, 1)
    cx.pM, cx.b_pM = cx.ps("pM", [128, 512], F32, 1)


def build_MLA(ctx_out):
    cx = Ctx(); kb = cx.kb
    zm = cx.din("zm", [T_B, 544]); ident_d = cx.din("ident", [128, 128])
    wuq = cx.din("wuq", [384, 384]); wukv = cx.din("wukv", [128, 512])
    qg = cx.din("qg", [1, 384]); kvg = cx.din("kvg", [1, 128])
    cq = cx.din("cq", [T_B, 32]); ck = cx.din("ck", [T_B, 32])
    y = cx.dout("y", [T_B, 256])
    attn_common_alloc(cx, 96)
    ident, b_ident = cx.sb("identf", [128, 128], F32)
    identb, b_identb = cx.sb("identb", [128, 128], BF16)
    kb.dma("sp", lambda e: e.dma_start(out=ident[:], in_=ident_d), [], [b_ident])
    kb.op("dve", lambda e: e.tensor_copy(out=identb[:], in_=ident[:]), [b_ident], [b_identb])
    wuqb, b_wuqb = cx.sb("wuqb", [128, 3, 384], BF16)
    wukvb, b_wukvb = cx.sb("wukvb", [128, 512], BF16)
    kb.dma("pool", lambda e: e.dma_start(out=wuqb[:], in_=wuq.rearrange("(c p) n -> p c n", p=128)), [], [b_wuqb])
    kb.dma("pool", lambda e: e.dma_start(out=wukvb[:], in_=wukv), [], [b_wukvb])
    qgv, b_qgv = cx.sb("qgv", [128, 384], F32); kvgv, b_kvgv = cx.sb("kvgv", [128, 128], F32)
    kb.dma("sp", lambda e: e.dma_start(out=qgv[:], in_=qg.partition_broadcast(128)), [], [b_qgv])
    kb.dma("sp", lambda e: e.dma_start(out=kvgv[:], in_=kvg.partition_broadcast(128)), [], [b_kvgv])
    QT = []; b_QT = []; KTs = []; b_KT = []
    for h in range(4):
        t, b = cx.sb("QT%d" % h, [96, T_B], BF16); QT.append(t); b_QT.append(b)
        t, b = cx.sb("KT%d" % h, [96, T_B], BF16); KTs.append(t); b_KT.append(b)
    Vp = []; b_Vp = []
    for h in range(4):
        t, b = cx.sb("Vp%d" % h, [128, NT_B, 65], BF16); Vp.append(t); b_Vp.append(b)
        kb.op("pool", lambda e, t=t: e.memset(t[:], 1.0), [], [b])
    zt, b_zt = cx.sb("zt", [128, 544], F32, 2)
    cqt, b_cqt = cx.sb("cqt", [128, 32], F32, 2); ckt, b_ckt = cx.sb("ckt", [128, 32], F32, 2)
    junk, b_junk = cx.sb("junk", [128, 512], F32)
    ss, b_ss = cx.sb("ss", [128, 4], F32)
    qn, b_qn = cx.sb("qn", [128, 512], F32)
    qnT, b_qnT = cx.sb("qnT", [128, 4, 128], BF16)
    Qtm, b_Qtm = cx.sb("Qtm", [128, 4, 96], BF16); Ktm, b_Ktm = cx.sb("Ktm", [128, 4, 96], BF16)
    kpe, b_kpe = cx.sb("kpe", [128, 32], F32)
    tmp, b_tmp = cx.sb("tmp", [128, 4, 16], F32, 2)
    pM, b_pM = cx.pM, cx.b_pM
    pKV, b_pKV = cx.ps("pKV", [128, 512], F32)
    qf, b_qf = cx.sb("qf", [128, 384], F32); kvf, b_kvf = cx.sb("kvf", [128, 512], F32)
    import os
    for tt in range(int(os.environ.get('DBG_TILES', NT_B))):
        zi = cx.nxt("zt", 2)
        Z = zt[zi]; CQ = cqt[zi]; CK = ckt[zi]
        kb.dma("sp", lambda e, tt=tt, Z=Z: e.dma_start(out=Z[:], in_=zm[tt * 128:(tt + 1) * 128, :]), [], [b_zt[zi]])
        kb.dma("sp", lambda e, tt=tt, CQ=CQ: e.dma_start(out=CQ[:], in_=cq[tt * 128:(tt + 1) * 128, :]), [], [b_cqt[zi]])
        kb.dma("sp", lambda e, tt=tt, CK=CK: e.dma_start(out=CK[:], in_=ck[tt * 128:(tt + 1) * 128, :]), [], [b_ckt[zi]])
        STEP = int(os.environ.get('DBG_STEP', 99))
        if STEP < 1: continue
        kb.op("act", lambda e, Z=Z: e.activation(out=junk[:, 0:512], in_=Z[:, 0:512], func=AF.Square), [b_zt[zi]], [b_junk])
        kb.op("dve", lambda e: e.reduce_sum(out=ss[:, 0:1], in_=junk[:, 0:384], axis=AX.X), [b_junk], [b_ss])
        kb.op("dve", lambda e: e.reduce_sum(out=ss[:, 1:2], in_=junk[:, 384:512], axis=AX.X), [b_junk], [b_ss])
        kb.op("act", lambda e: e.activation(out=ss[:, 2:3], in_=ss[:, 0:1], func=AF.Sqrt, bias=1e-6, scale=1.0 / 384), [b_ss], [b_ss])
        kb.op("act", lambda e: e.activation(out=ss[:, 3:4], in_=ss[:, 1:2], func=AF.Sqrt, bias=1e-6, scale=1.0 / 128), [b_ss], [b_ss])
        kb.op("dve", lambda e: e.reciprocal(out=ss[:, 2:4], in_=ss[:, 2:4]), [b_ss], [b_ss])
        kb.op("dve", lambda e, Z=Z: e.scalar_tensor_tensor(out=qn[:, 0:384], in0=Z[:, 0:384], scalar=ss[:, 2:3], in1=qgv[:], op0=ALU.mult, op1=ALU.mult),
              [b_zt[zi], b_ss, b_qgv], [b_qn])
        kb.op("dve", lambda e, Z=Z: e.scalar_tensor_tensor(out=qn[:, 384:512], in0=Z[:, 384:512], scalar=ss[:, 3:4], in1=kvgv[:], op0=ALU.mult, op1=ALU.mult),
              [b_zt[zi], b_ss, b_kvgv], [b_qn])
        if STEP < 2: continue
        for i in range(4):
            kb.op("pe", lambda e, i=i: e.transpose(out=cx.pTf[:, i, :], in_=qn[:, i * 128:(i + 1) * 128], identity=ident[:]), [b_qn, b_ident], [cx.b_pTf])
        kb.op("act", lambda e: e.copy(out=qnT[:], in_=cx.pTf[:]), [cx.b_pTf], [b_qnT])
        for kc in range(3):
            kb.op("pe", lambda e, kc=kc: e.matmul(pM[:, 0:384], lhsT=qnT[:, kc, :], rhs=wuqb[:, kc, :], start=(kc == 0), stop=(kc == 2)), [b_qnT, b_wuqb], [b_pM])
        kb.op("pe", lambda e: e.matmul(pKV[:], lhsT=qnT[:, 3, :], rhs=wukvb[:], start=True, stop=True), [b_qnT, b_wukvb], [b_pKV])
        if STEP < 3: continue
        kb.op("act", lambda e: e.copy(out=qf[:], in_=pM[:, 0:384]), [b_pM], [b_qf])
        kb.op("dve", lambda e: e.tensor_copy(out=kvf[:], in_=pKV[:]), [b_pKV], [b_kvf])
        pq = qf[:].rearrange("p (h d) -> p h d", d=96)
        pkv = kvf[:].rearrange("p (h d) -> p h d", d=128)
        s = 96.0 ** -0.5
        kb.op("act", lambda e, pq=pq: e.mul(out=Qtm[:, :, 0:64], in_=pq[:, :, 0:64], mul=s), [b_qf], [b_Qtm])
        SUB = int(os.environ.get('DBG_SUB', 99))
        if SUB < 1: continue
        cqc = CQ[:, 0:16].unsqueeze(1).broadcast_to([128, 4, 16]); cqs = CQ[:, 16:32].unsqueeze(1).broadcast_to([128, 4, 16])
        T0, T1 = tmp[0], tmp[1]
        kb.op("dve", lambda e, pq=pq, cqc=cqc: e.tensor_tensor(out=T0[:], in0=pq[:, :, 64:80], in1=cqc, op=ALU.mult), [b_qf, b_cqt[zi]], [b_tmp[0]])
        kb.op("dve", lambda e, pq=pq, cqs=cqs: e.tensor_tensor(out=T1[:], in0=pq[:, :, 80:96], in1=cqs, op=ALU.mult), [b_qf, b_cqt[zi]], [b_tmp[1]])
        if SUB < 2: continue
        kb.op("dve", lambda e: e.tensor_tensor(out=Qtm[:, :, 64:80], in0=T0[:], in1=T1[:], op=ALU.subtract), [b_tmp[0], b_tmp[1]], [b_Qtm], accum=True)
        kb.op("dve", lambda e, pq=pq, cqc=cqc: e.tensor_tensor(out=T0[:], in0=pq[:, :, 80:96], in1=cqc, op=ALU.mult), [b_qf, b_cqt[zi]], [b_tmp[0]])
        kb.op("dve", lambda e, pq=pq, cqs=cqs: e.tensor_tensor(out=T1[:], in0=pq[:, :, 64:80], in1=cqs, op=ALU.mult), [b_qf, b_cqt[zi]], [b_tmp[1]])
        kb.op("dve", lambda e: e.tensor_tensor(out=Qtm[:, :, 80:96], in0=T0[:], in1=T1[:], op=ALU.add), [b_tmp[0], b_tmp[1]], [b_Qtm], accum=True)
        if STEP < 4: continue
        kb.op("act", lambda e, pkv=pkv: e.copy(out=Ktm[:, :, 0:64], in_=pkv[:, :, 0:64]), [b_kvf], [b_Ktm])
        for hh in range(4):
            kb.op("act", lambda e, pkv=pkv, hh=hh, tt=tt: e.copy(out=Vp[hh][:, tt, 0:64], in_=pkv[:, hh, 64:128]), [b_kvf, b_Vp[hh]], [b_Vp[hh]], accum=True)
        A0 = T0[:, 0, :]; A1 = T1[:, 0, :]
        kb.op("dve", lambda e, Z=Z, CK=CK: e.tensor_tensor(out=A0, in0=Z[:, 512:528], in1=CK[:, 0:16], op=ALU.mult), [b_zt[zi], b_ckt[zi]], [b_tmp[0]])
        kb.op("dve", lambda e, Z=Z, CK=CK: e.tensor_tensor(out=A1, in0=Z[:, 528:544], in1=CK[:, 16:32], op=ALU.mult), [b_zt[zi], b_ckt[zi]], [b_tmp[1]])
        kb.op("dve", lambda e: e.tensor_tensor(out=kpe[:, 0:16], in0=A0, in1=A1, op=ALU.subtract), [b_tmp[0], b_tmp[1]], [b_kpe])
        kb.op("dve", lambda e, Z=Z, CK=CK: e.tensor_tensor(out=A0, in0=Z[:, 528:544], in1=CK[:, 0:16], op=ALU.mult), [b_zt[zi], b_ckt[zi]], [b_tmp[0]])
        kb.op("dve", lambda e, Z=Z, CK=CK: e.tensor_tensor(out=A1, in0=Z[:, 512:528], in1=CK[:, 16:32], op=ALU.mult), [b_zt[zi], b_ckt[zi]], [b_tmp[1]])
        kb.op("dve", lambda e: e.tensor_tensor(out=kpe[:, 16:32], in0=A0, in1=A1, op=ALU.add), [b_tmp[0], b_tmp[1]], [b_kpe], accum=True)
        for hh in range(4):
            kb.op("dve", lambda e, hh=hh: e.tensor_copy(out=Ktm[:, hh, 64:96], in_=kpe[:]), [b_kpe], [b_Ktm], accum=True)
        if STEP < 5: continue
        p = 0
        for hh in range(4):
            kb.op("pe", lambda e, p=p, hh=hh: e.transpose(out=cx.pT[p][0:96, hh, :], in_=Qtm[:, hh, :], identity=identb[:]), [b_Qtm, b_identb], [cx.b_pT[p]])
        for hh in range(4):
            kb.op("act", lambda e, p=p, hh=hh, tt=tt: e.copy(out=QT[hh][:, tt * 128:(tt + 1) * 128], in_=cx.pT[p][0:96, hh, :]), [cx.b_pT[p]], [b_QT[hh]], accum=True)
        p = 0
        for hh in range(4):
            kb.op("pe", lambda e, p=p, hh=hh: e.transpose(out=cx.pT[p][0:96, hh, :], in_=Ktm[:, hh, :], identity=identb[:]), [b_Ktm, b_identb], [cx.b_pT[p]])
        for hh in range(4):
            kb.op("dve", lambda e, p=p, hh=hh, tt=tt: e.tensor_copy(out=KTs[hh][:, tt * 128:(tt + 1) * 128], in_=cx.pT[p][0:96, hh, :]), [cx.b_pT[p]], [b_KT[hh]], accum=True)
    nq_t = NT_B if ctx_out else 32
    ybuf = []; b_ybuf = []
    yb_all, b_yb = cx.sb("yb", [128, NT_B, 256], F32)
    for qt in range(NT_B):
        ybuf.append(yb_all[:, qt, :]); b_ybuf.append(Buf())
    all_keys = [(kt, None, None) for kt in range(NT_B)]
    ctx_keys = [(kt, None, None) for kt in (32, 33)]
    for h in range(int(os.environ.get('DBG_HEADS', 4))):
        jobs = [(qb * 128, 128, all_keys) for qb in range(32)]
        if ctx_out:
            jobs += [(4096, 128, ctx_keys), (4224, 128, ctx_keys)]
        attention_core(cx, QT[h], b_QT[h], KTs[h], b_KT[h], Vp[h], b_Vp[h], 96, jobs, ybuf, b_ybuf, h)
    b_out = Buf()
    for qt in range(nq_t):
        kb.dma("sp", lambda e, qt=qt: e.dma_start(out=y[qt * 128:(qt + 1) * 128, :], in_=ybuf[qt]), [b_ybuf[qt]], [b_out], accum=(qt > 0))
    kb.finish([b_out])
    kb.emit()
    return cx.nc


def build_SWA(ctx_out):
    cx = Ctx(); kb = cx.kb
    zs = cx.din("zs", [T_B, 384]); ident_d = cx.din("ident", [128, 128])
    cq = cx.din("cq", [T_B, 64]); ck = cx.din("ck", [T_B, 64])
    mprev_d = cx.din("mprev", [128, 128]); mnext_d = cx.din("mnext", [128, 128])
    sink_d = cx.din("sink", [1, 4])
    y = cx.dout("y", [T_B, 256])
    attn_common_alloc(cx, 64)
    ident, b_ident = cx.sb("identf", [128, 128], F32)
    identb, b_identb = cx.sb("identb", [128, 128], BF16)
    kb.dma("sp", lambda e: e.dma_start(out=ident[:], in_=ident_d), [], [b_ident])
    kb.op("dve", lambda e: e.tensor_copy(out=identb[:], in_=ident[:]), [b_ident], [b_identb])
    mprev, b_mprev = cx.sb("mprev_sb", [128, 128], F32); mnext, b_mnext = cx.sb("mnext_sb", [128, 128], F32)
    kb.dma("sp", lambda e: e.dma_start(out=mprev[:], in_=mprev_d), [], [b_mprev])
    kb.dma("sp", lambda e: e.dma_start(out=mnext[:], in_=mnext_d), [], [b_mnext])
    es, b_es = cx.sb("es", [128, 4], F32)
    kb.dma("sp", lambda e: e.dma_start(out=es[:], in_=sink_d.partition_broadcast(128)), [], [b_es])
    kb.op("act", lambda e: e.activation(out=es[:], in_=es[:], func=AF.Exp), [b_es], [b_es])
    QT = []; b_QT = []
    for h in range(4):
        t, b = cx.sb("QT%d" % h, [64, T_B], BF16); QT.append(t); b_QT.append(b)
    KT, b_KT = cx.sb("KT", [64, T_B], BF16)
    Vp, b_Vp = cx.sb("Vp", [128, NT_B, 65], BF16)
    kb.op("pool", lambda e: e.memset(Vp[:], 1.0), [], [b_Vp])
    zt, b_zt = cx.sb("zt", [128, 384], F32, 2)
    cqt, b_cqt = cx.sb("cqt", [128, 64], F32, 2); ckt, b_ckt = cx.sb("ckt", [128, 64], F32, 2)
    Qtm, b_Qtm = cx.sb("Qtm", [128, 4, 64], BF16); Ktm, b_Ktm = cx.sb("Ktm", [128, 64], BF16)
    tmp, b_tmp = cx.sb("tmp", [128, 4, 32], F32, 2)
    T0, T1 = tmp
    for tt in range(NT_B):
        zi = cx.nxt("zt", 2)
        Z = zt[zi]; CQ = cqt[zi]; CK = ckt[zi]
        kb.dma("sp", lambda e, tt=tt, Z=Z: e.dma_start(out=Z[:], in_=zs[tt * 128:(tt + 1) * 128, :]), [], [b_zt[zi]])
        kb.dma("sp", lambda e, tt=tt, CQ=CQ: e.dma_start(out=CQ[:], in_=cq[tt * 128:(tt + 1) * 128, :]), [], [b_cqt[zi]])
        kb.dma("sp", lambda e, tt=tt, CK=CK: e.dma_start(out=CK[:], in_=ck[tt * 128:(tt + 1) * 128, :]), [], [b_ckt[zi]])
        pq = Z[:, 0:256].rearrange("p (h d) -> p h d", d=64)
        cqc = CQ[:, 0:32].unsqueeze(1).broadcast_to([128, 4, 32]); cqs = CQ[:, 32:64].unsqueeze(1).broadcast_to([128, 4, 32])
        rd = [b_zt[zi], b_cqt[zi]]
        kb.op("dve", lambda e, pq=pq, cqc=cqc: e.tensor_tensor(out=T0[:], in0=pq[:, :, 0:32], in1=cqc, op=ALU.mult), rd, [b_tmp[0]])
        kb.op("dve", lambda e, pq=pq, cqs=cqs: e.tensor_tensor(out=T1[:], in0=pq[:, :, 32:64], in1=cqs, op=ALU.mult), rd, [b_tmp[1]])
        kb.op("dve", lambda e: e.tensor_tensor(out=Qtm[:, :, 0:32], in0=T0[:], in1=T1[:], op=ALU.subtract), [b_tmp[0], b_tmp[1]], [b_Qtm])
        kb.op("dve", lambda e, pq=pq, cqc=cqc: e.tensor_tensor(out=T0[:], in0=pq[:, :, 32:64], in1=cqc, op=ALU.mult), rd, [b_tmp[0]])
        kb.op("dve", lambda e, pq=pq, cqs=cqs: e.tensor_tensor(out=T1[:], in0=pq[:, :, 0:32], in1=cqs, op=ALU.mult), rd, [b_tmp[1]])
        kb.op("dve", lambda e: e.tensor_tensor(out=Qtm[:, :, 32:64], in0=T0[:], in1=T1[:], op=ALU.add), [b_tmp[0], b_tmp[1]], [b_Qtm], accum=True)
        A0 = T0[:, 0, :]; A1 = T1[:, 0, :]
        rk = [b_zt[zi], b_ckt[zi]]
        kb.op("dve", lambda e, Z=Z, CK=CK: e.tensor_tensor(out=A0, in0=Z[:, 256:288], in1=CK[:, 0:32], op=ALU.mult), rk, [b_tmp[0]])
        kb.op("dve", lambda e, Z=Z, CK=CK: e.tensor_tensor(out=A1, in0=Z[:, 288:320], in1=CK[:, 32:64], op=ALU.mult), rk, [b_tmp[1]])
        kb.op("dve", lambda e: e.tensor_tensor(out=Ktm[:, 0:32], in0=A0, in1=A1, op=ALU.subtract), [b_tmp[0], b_tmp[1]], [b_Ktm])
        kb.op("dve", lambda e, Z=Z, CK=CK: e.tensor_tensor(out=A0, in0=Z[:, 288:320], in1=CK[:, 0:32], op=ALU.mult), rk, [b_tmp[0]])
        kb.op("dve", lambda e, Z=Z, CK=CK: e.tensor_tensor(out=A1, in0=Z[:, 256:288], in1=CK[:, 32:64], op=ALU.mult), rk, [b_tmp[1]])
        kb.op("dve", lambda e: e.tensor_tensor(out=Ktm[:, 32:64], in0=A0, in1=A1, op=ALU.add), [b_tmp[0], b_tmp[1]], [b_Ktm], accum=True)
        kb.op("act", lambda e, Z=Z, tt=tt: e.copy(out=Vp[:, tt, 0:64], in_=Z[:, 320:384]), [b_zt[zi], b_Vp], [b_Vp], accum=True)
        p = 0
        for hh in range(4):
            kb.op("pe", lambda e, p=p, hh=hh: e.transpose(out=cx.pT[p][0:64, hh, :], in_=Qtm[:, hh, :], identity=identb[:]), [b_Qtm, b_identb], [cx.b_pT[p]])
        for hh in range(4):
            kb.op("act", lambda e, p=p, hh=hh, tt=tt: e.copy(out=QT[hh][:, tt * 128:(tt + 1) * 128], in_=cx.pT[p][0:64, hh, :]), [cx.b_pT[p]], [b_QT[hh]], accum=True)
        p = 0
        kb.op("pe", lambda e, p=p: e.transpose(out=cx.pT[p][0:64, 0, :], in_=Ktm[:], identity=identb[:]), [b_Ktm, b_identb], [cx.b_pT[p]])
        kb.op("dve", lambda e, p=p, tt=tt: e.tensor_copy(out=KT[:, tt * 128:(tt + 1) * 128], in_=cx.pT[p][0:64, 0, :]), [cx.b_pT[p]], [b_KT], accum=True)
    nq_t = NT_B if ctx_out else 32
    yb_all, b_yb = cx.sb("yb", [128, NT_B, 256], F32)
    ybuf = [yb_all[:, qt, :] for qt in range(NT_B)]; b_ybuf = [Buf() for _ in range(NT_B)]
    for h in range(4):
        jobs = []
        for n in range(32):
            keys = []
            if n > 0:
                keys.append((n - 1, mprev[:], b_mprev))
            keys.append((n, None, None))
            if n < 31:
                keys.append((n + 1, mnext[:], b_mnext))
            keys += [(32, None, None), (33, None, None)]
            jobs.append((n * 128, 128, keys))
        if ctx_out:
            for i in range(2):
                jobs.append((4096 + i * 128, 128, [(32, None, None), (33, None, None)]))
        attention_core(cx, QT[h], b_QT[h], KT, b_KT, Vp, b_Vp, 64, jobs, ybuf, b_ybuf, h, extra_den=es[:, h:h + 1], b_extra=b_es)
    b_out = Buf()
    for qt in range(nq_t):
        kb.dma("sp", lambda e, qt=qt: e.dma_start(out=y[qt * 128:(qt + 1) * 128, :], in_=ybuf[qt]), [b_ybuf[qt]], [b_out], accum=(qt > 0))
    kb.finish([b_out])
    kb.emit()
    return cx.nc


def build_HY(L):
    cx = Ctx(); kb = cx.kb
    NT = L // 128
    NF = NT + 1
    NP = NF * 128
    BLK = min(512, L)
    zh = cx.din("zh", [L, 768]); cw = cx.din("cw", [3, 768]); cb = cx.din("cb", [1, 768])
    featsT = cx.din("featsT", [33, L]); w1 = cx.din("w1", [33, 64]); w2 = cx.din("w2", [64, 64])
    pv = cx.din("pv", [64, 4])
    w3c = cx.din("w3c", [64, 1024]); dwf = cx.din("dwf", [L, 256]); dwb = cx.din("dwb", [L, 256])
    hb = cx.din("hb", [1, 512]); wfd = cx.din("wf", [128, NF])
    Cc = cx.din("Cc", [NP, NP]); Ss = cx.din("Ss", [NP, NP])
    y = cx.dout("y", [L, 256])
    x12 = cx.dscr("x12", [L, 512]); b_x12 = [Buf() for _ in range(NT)]
    Gd = cx.dscr("Gd", [NP, 512]); b_Gd = [Buf() for _ in range(NF)]
    U, b_U = cx.sb("U", [128, NT, 256], F32); b_Ut = [Buf() for _ in range(NT)]
    AB, b_AB = cx.sb("AB", [128, NT, 2, 256], F32); b_ABt = [Buf() for _ in range(NT)]
    RG, b_RG = cx.sb("RG", [128, max(NF * 512, 6144)], F32)
    PQ = RG[:, 0:NF * 512].rearrange("p (f a c) -> p f a c", a=2, c=256); b_PQt = [Buf() for _ in range(NF)]
    tc_, b_tc = cx.sb("tcs", [128, 384], F32, 2); ts_, b_ts = cx.sb("tss", [128, 384], F32, 2)
    cwv = RG[:, 0:2304].rearrange("p (a c) -> p a c", a=3); b_cwv = Buf()
    cbv = RG[:, 2304:3072]; b_cbv = Buf()
    hbv, b_hbv = cx.sb("hbv", [128, 512], F32); wfs, b_wfs = cx.sb("wfs", [128, NF], F32)
    ones, b_ones = cx.sb("ones", [128, 128], F32)
    w1s, b_w1s = cx.sb("w1s", [33, 64], F32); w2s, b_w2s = cx.sb("w2s", [64, 64], F32); pvs, b_pvs = cx.sb("pvs", [64, 4], F32)
    w3s, b_w3s = cx.sb("w3s", [64, 1024], F32)
    fT, b_fT = cx.sb("fT", [33, BLK], F32)
    h1, b_h1 = cx.sb("h1", [64, BLK], F32); h2, b_h2 = cx.sb("h2", [64, BLK], F32)
    sa, b_sa = cx.sb("sa", [64, BLK], F32); sk, b_sk = cx.sb("sk", [64, BLK], mybir.dt.int32)
    zt = [RG[:, 3072:5376].rearrange("p (a c) -> p a c", a=3)]; b_zt = [Buf()]
    acc = RG[:, 5376:6144]; b_acc = Buf()
    dft, b_dft = cx.sb("dft", [128, 2, 256], F32, 2)
    fb, b_fb = cx.sb("fb", [128, 2, 256], F32); ab_, b_ab = cx.sb("ab", [128, 2, 256], F32)
    rinv, b_rinv = cx.sb("rinv", [128, 256], F32)
    gt, b_gt = cx.sb("gt", [128, 512], F32, 2); xt_, b_xt = cx.sb("xtile", [128, 256], F32, 2)
    e0, b_e0 = cx.sb("e0", [128, 256], F32); e1, b_e1 = cx.sb("e1", [128, 256], F32)
    go, b_go = cx.sb("go", [128, 512], F32, 1, aslist=True)
    pPf, b_pP = cx.ps("pP", [128, 512], F32, 3); pQf, b_pQ = cx.ps("pQ", [128, 512], F32, 3)
    pP = [t[:, 0:256] for t in pPf]; pQ = [t[:, 0:256] for t in pQf]
    pF0, b_pF0 = cx.ps("pF", [64, 512], F32)
    pF = [pF0, pF0]; b_pF = [b_pF0, b_pF0]
    pH, b_pH = cx.ps("pH", [128, 512], F32)
    pN = pPf[0][:, 256:512]; b_pN = b_pP[0]
    nrm, b_nrm = cx.sb("nrm", [128, 256], F32)
    for i in range(3):
        kb.dma("sp", lambda e, i=i: e.dma_start(out=cwv[:, i, :], in_=cw[i:i + 1, :].partition_broadcast(128)), [], [b_cwv], accum=(i > 0))
    kb.dma("sp", lambda e: e.dma_start(out=cbv, in_=cb.partition_broadcast(128)), [], [b_cbv])
    kb.dma("sp", lambda e: e.dma_start(out=hbv[:], in_=hb.partition_broadcast(128)), [], [b_hbv])
    kb.dma("sp", lambda e: e.dma_start(out=wfs[:], in_=wfd), [], [b_wfs])
    kb.dma("sp", lambda e: e.dma_start(out=w1s[:], in_=w1), [], [b_w1s])
    kb.dma("sp", lambda e: e.dma_start(out=w2s[:], in_=w2), [], [b_w2s])
    kb.dma("sp", lambda e: e.dma_start(out=pvs[:], in_=pv), [], [b_pvs])
    kb.dma("sp", lambda e: e.dma_start(out=w3s[:], in_=w3c), [], [b_w3s])
    kb.op("pool", lambda e: e.memset(ones[:], 1.0), [], [b_ones])
    hpi, b_hpi = cx.sb("hpi", [128, 1], F32)
    kb.op("pool", lambda e: e.memset(hpi[:], math.pi / 2), [], [b_hpi])
    kb.op("dve", lambda e: e.tensor_tensor(out=pvs[:, 0:2], in0=pvs[:, 0:2], in1=pvs[:, 2:4], op=ALU.mult), [b_pvs], [b_pvs])

    for tt in range(NT):
        zi = 0; Z = zt[zi]
        r0 = tt * 128
        if tt == 0 or tt == NT - 1:
            kb.op("pool", lambda e, Z=Z: e.memset(Z[:], 0.0), [], [b_zt[zi]])
        acc_flag = (tt == 0 or tt == NT - 1)
        if tt == 0:
            kb.dma("sp", lambda e, Z=Z: e.dma_start(out=Z[1:128, 0, :], in_=zh[0:127, :]), [b_zt[zi]], [b_zt[zi]], accum=True)
        else:
            kb.dma("sp", lambda e, Z=Z, r0=r0: e.dma_start(out=Z[:, 0, :], in_=zh[r0 - 1:r0 + 127, :]), [b_zt[zi]] if acc_flag else [], [b_zt[zi]], accum=acc_flag)
        kb.dma("sp", lambda e, Z=Z, r0=r0: e.dma_start(out=Z[:, 1, :], in_=zh[r0:r0 + 128, :]), [b_zt[zi]] if acc_flag else [], [b_zt[zi]], accum=True)
        if tt == NT - 1:
            kb.dma("sp", lambda e, Z=Z, r0=r0: e.dma_start(out=Z[0:127, 2, :], in_=zh[r0 + 1:r0 + 128, :]), [b_zt[zi]], [b_zt[zi]], accum=True)
        else:
            kb.dma("sp", lambda e, Z=Z, r0=r0: e.dma_start(out=Z[:, 2, :], in_=zh[r0 + 1:r0 + 129, :]), [b_zt[zi]] if acc_flag else [], [b_zt[zi]], accum=True)
        kb.op("dve", lambda e, Z=Z: e.tensor_tensor(out=Z[:], in0=Z[:], in1=cwv, op=ALU.mult), [b_zt[zi], b_cwv], [b_zt[zi]])
        kb.op("dve", lambda e, Z=Z: e.tensor_tensor(out=acc, in0=Z[:, 0, :], in1=Z[:, 1, :], op=ALU.add), [b_zt[zi]], [b_acc])
        kb.op("dve", lambda e, Z=Z: e.tensor_tensor(out=acc, in0=acc, in1=Z[:, 2, :], op=ALU.add), [b_zt[zi], b_acc], [b_acc])
        kb.op("dve", lambda e, tt=tt: e.tensor_tensor(out=U[:, tt, :], in0=acc[:, 0:256], in1=cbv[:, 0:256], op=ALU.add), [b_acc, b_cbv], [b_Ut[tt]])
        kb.op("dve", lambda e: e.tensor_tensor(out=acc[:, 256:768], in0=acc[:, 256:768], in1=cbv[:, 256:768], op=ALU.add), [b_acc, b_cbv], [b_acc])
        kb.dma("sp", lambda e, r0=r0: e.dma_start(out=x12[r0:r0 + 128, :], in_=acc[:, 256:768]), [b_acc], [b_x12[tt]])

    def sin_step(ps, col, dst, b_dst, n):
        kb.op("act", lambda e: e.activation(out=sa[:, 0:n], in_=ps, func=AF.Identity, bias=pvs[:, col:col + 1], scale=pvs[:, 2 + col:3 + col]), [b_pF[col], b_pvs], [b_sa])
        kb.op("dve", lambda e: e.tensor_scalar(out=sa[:, 0:n], in0=sa[:, 0:n], scalar1=1.0 / (2 * math.pi), scalar2=32.0, op0=ALU.mult, op1=ALU.add), [b_sa], [b_sa])
        kb.op("dve", lambda e: e.tensor_copy(out=sk[:, 0:n], in_=sa[:, 0:n]), [b_sa], [b_sk])
        kb.op("dve", lambda e: e.tensor_copy(out=dst[:, 0:n], in_=sk[:, 0:n]), [b_sk], [b_dst])
        kb.op("dve", lambda e: e.tensor_tensor(out=sa[:, 0:n], in0=sa[:, 0:n], in1=dst[:, 0:n], op=ALU.subtract), [b_sa, b_dst], [b_sa])
        kb.op("act", lambda e: e.activation(out=dst[:, 0:n], in_=sa[:, 0:n], func=AF.Sin, scale=math.pi), [b_sa], [b_dst])
        kb.op("act", lambda e: e.activation(out=sa[:, 0:n], in_=sa[:, 0:n], func=AF.Sin, scale=-math.pi, bias=hpi[0:64, 0:1]), [b_sa, b_hpi], [b_sa])
        kb.op("dve", lambda e: e.scalar_tensor_tensor(out=dst[:, 0:n], in0=dst[:, 0:n], scalar=2.0, in1=sa[:, 0:n], op0=ALU.mult, op1=ALU.mult), [b_sa, b_dst], [b_dst])

    def table_blocks(r, c0, ncols):
        i = cx.nxt("tb", 2)
        kb.dma("sp", lambda e: e.dma_start(out=tc_[i][:, 0:ncols], in_=Cc[r * 128:(r + 1) * 128, c0:c0 + ncols]), [], [b_tc[i]])
        kb.dma("sp", lambda e: e.dma_start(out=ts_[i][:, 0:ncols], in_=Ss[r * 128:(r + 1) * 128, c0:c0 + ncols]), [], [b_ts[i]])
        return i

    for o in range(2):
        for blk in range(L // BLK):
            l0 = blk * BLK
            kb.dma("sp", lambda e, l0=l0: e.dma_start(out=fT[:], in_=featsT[:, l0:l0 + BLK]), [], [b_fT])
            kb.op("pe", lambda e: e.matmul(pF0[:, 0:BLK], lhsT=w1s[:], rhs=fT[:], start=True, stop=True), [b_w1s, b_fT], [b_pF[0]])
            sin_step(pF0[:, 0:BLK], 0, h1, b_h1, BLK)
            kb.op("pe", lambda e: e.matmul(pF0[:, 0:BLK], lhsT=w2s[:], rhs=h1[:], start=True, stop=True), [b_w2s, b_h1], [b_pF[1]])
            sin_step(pF0[:, 0:BLK], 1, h2, b_h2, BLK)
            for j in range(BLK // 128):
                tt = blk * (BLK // 128) + j
                di = cx.nxt("dft", 2)
                kb.dma("sp", lambda e, tt=tt, di=di: e.dma_start(out=dft[di][:, 0, :], in_=dwf[tt * 128:(tt + 1) * 128, :]), [], [b_dft[di]])
                kb.dma("sp", lambda e, tt=tt, di=di: e.dma_start(out=dft[di][:, 1, :], in_=dwb[tt * 128:(tt + 1) * 128, :]), [], [b_dft[di]], accum=True)
                kb.op("pe", lambda e, j=j, o=o: e.matmul(pH[:], lhsT=h2[:, j * 128:(j + 1) * 128], rhs=w3s[:, o * 512:(o + 1) * 512], start=True, stop=True), [b_h2, b_w3s], [b_pH])
                kb.op("dve", lambda e, di=di: e.tensor_tensor(out=fb[:], in0=pH[:].rearrange("p (a c) -> p a c", a=2), in1=dft[di][:], op=ALU.mult), [b_pH, b_dft[di]], [b_fb])
                kb.op("dve", lambda e, tt=tt: e.tensor_tensor(out=AB[:, tt, 0, :], in0=fb[:, 0, :], in1=fb[:, 1, :], op=ALU.add), [b_fb], [b_ABt[tt]])
                kb.op("dve", lambda e, tt=tt: e.tensor_tensor(out=AB[:, tt, 1, :], in0=fb[:, 1, :], in1=fb[:, 0, :], op=ALU.subtract), [b_fb], [b_ABt[tt]], accum=True)
                kb.op("act", lambda e: e.activation(out=ab_[:], in_=fb[:], func=AF.Abs), [b_fb], [b_ab])
                for a in range(2):
                    kb.op("pe", lambda e, a=a: e.matmul(pN, lhsT=ones[:], rhs=ab_[:, a, :], start=(a == 0), stop=(a == 1)), [b_ones, b_ab], [b_pN])
                if tt == 0:
                    kb.op("dve", lambda e: e.tensor_copy(out=nrm[:], in_=pN), [b_pN], [b_nrm])
                else:
                    kb.op("dve", lambda e: e.tensor_tensor(out=nrm[:], in0=nrm[:], in1=pN, op=ALU.add), [b_pN, b_nrm], [b_nrm])
        kb.op("dve", lambda e: e.reciprocal(out=rinv[:], in_=nrm[:]), [b_nrm], [b_rinv])
        for fg in range((NF + 2) // 3):
            f0 = fg * 3; nf = min(3, NF - f0)
            for tcn in range(NT):
                i = table_blocks(tcn, f0 * 128, nf * 128)
                for j in range(nf):
                    kb.op("pe", lambda e, i=i, j=j, tcn=tcn: e.matmul(pP[j], lhsT=tc_[i][:, j * 128:(j + 1) * 128], rhs=AB[:, tcn, 0, :], start=(tcn == 0), stop=(tcn == NT - 1)),
                          [b_tc[i], b_ABt[tcn]], [b_pP[j]])
                    kb.op("pe", lambda e, i=i, j=j, tcn=tcn: e.matmul(pQ[j], lhsT=ts_[i][:, j * 128:(j + 1) * 128], rhs=AB[:, tcn, 1, :], start=(tcn == 0), stop=(tcn == NT - 1)),
                          [b_ts[i], b_ABt[tcn]], [b_pQ[j]])
            for j in range(nf):
                f = f0 + j
                gi = 0
                kb.op("dve", lambda e, j=j, f=f, gi=gi: e.scalar_tensor_tensor(out=go[gi][:, 0:256], in0=pP[j], scalar=wfs[:, f:f + 1], in1=rinv[:], op0=ALU.mult, op1=ALU.mult),
                      [b_pP[j], b_wfs, b_rinv], [b_go[gi]])
                kb.op("dve", lambda e, j=j, f=f, gi=gi: e.scalar_tensor_tensor(out=go[gi][:, 256:512], in0=pQ[j], scalar=wfs[:, f:f + 1], in1=rinv[:], op0=ALU.mult, op1=ALU.mult),
                      [b_pQ[j], b_wfs, b_rinv], [b_go[gi]], accum=True)
                kb.dma("sp", lambda e, f=f, gi=gi: e.dma_start(out=Gd[f * 128:(f + 1) * 128, :], in_=go[gi][:]), [b_go[gi]], [b_Gd[f]])
        for fg in range((NF + 2) // 3):
            f0 = fg * 3; nf = min(3, NF - f0)
            for tcn in range(NT):
                i = table_blocks(tcn, f0 * 128, nf * 128)
                for j in range(nf):
                    kb.op("pe", lambda e, i=i, j=j, tcn=tcn: e.matmul(pP[j], lhsT=tc_[i][:, j * 128:(j + 1) * 128], rhs=U[:, tcn, :], start=(tcn == 0), stop=(tcn == NT - 1)),
                          [b_tc[i], b_Ut[tcn]], [b_pP[j]])
                    kb.op("pe", lambda e, i=i, j=j, tcn=tcn: e.matmul(pQ[j], lhsT=ts_[i][:, j * 128:(j + 1) * 128], rhs=U[:, tcn, :], start=(tcn == 0), stop=(tcn == NT - 1)),
                          [b_ts[i], b_Ut[tcn]], [b_pQ[j]])
            for j in range(nf):
                f = f0 + j
                gi = cx.nxt("gt", 2); G = gt[gi]
                kb.dma("sp", lambda e, f=f, G=G: e.dma_start(out=G[:], in_=Gd[f * 128:(f + 1) * 128, :]), [b_Gd[f]], [b_gt[gi]])
                kb.op("dve", lambda e, j=j, G=G: e.tensor_tensor(out=e0[:], in0=pP[j], in1=G[:, 0:256], op=ALU.mult), [b_pP[j], b_gt[gi]], [b_e0])
                kb.op("dve", lambda e, j=j, G=G: e.tensor_tensor(out=e1[:], in0=pQ[j], in1=G[:, 256:512], op=ALU.mult), [b_pQ[j], b_gt[gi]], [b_e1])
                kb.op("dve", lambda e, f=f: e.tensor_tensor(out=PQ[:, f, 0, :], in0=e0[:], in1=e1[:], op=ALU.add), [b_e0, b_e1], [b_PQt[f], b_acc, b_zt[0], b_cwv, b_cbv])
                kb.op("dve", lambda e, j=j, G=G: e.tensor_tensor(out=e0[:], in0=pQ[j], in1=G[:, 0:256], op=ALU.mult), [b_pQ[j], b_gt[gi]], [b_e0])
                kb.op("dve", lambda e, j=j, G=G: e.tensor_tensor(out=e1[:], in0=pP[j], in1=G[:, 256:512], op=ALU.mult), [b_pP[j], b_gt[gi]], [b_e1])
                kb.op("dve", lambda e, f=f: e.tensor_tensor(out=PQ[:, f, 1, :], in0=e0[:], in1=e1[:], op=ALU.subtract), [b_e0, b_e1], [b_PQt[f]], accum=True)
        for tg in range((NT + 2) // 3):
            t0 = tg * 3; nt = min(3, NT - t0)
            for f in range(NF):
                i = table_blocks(f, t0 * 128, nt * 128)
                for j in range(nt):
                    kb.op("pe", lambda e, i=i, j=j, f=f: e.matmul(pP[j], lhsT=tc_[i][:, j * 128:(j + 1) * 128], rhs=PQ[:, f, 0, :], start=(f == 0), stop=False),
                          [b_tc[i], b_PQt[f]], [b_pP[j]])
                    kb.op("pe", lambda e, i=i, j=j, f=f: e.matmul(pP[j], lhsT=ts_[i][:, j * 128:(j + 1) * 128], rhs=PQ[:, f, 1, :], start=False, stop=(f == NF - 1)),
                          [b_ts[i], b_PQt[f]], [b_pP[j]])
            for j in range(nt):
                tt = t0 + j
                xi = cx.nxt("xtile", 2); X = xt_[xi]
                kb.dma("sp", lambda e, tt=tt, X=X, o=o: e.dma_start(out=X[:], in_=x12[tt * 128:(tt + 1) * 128, o * 256:(o + 1) * 256]), [b_x12[tt]], [b_xt[xi]])
                kb.op("dve", lambda e, tt=tt, o=o: e.tensor_tensor(out=e0[:], in0=U[:, tt, :], in1=hbv[:, o * 256:(o + 1) * 256], op=ALU.mult), [b_Ut[tt], b_hbv], [b_e0])
                kb.op("dve", lambda e, j=j: e.tensor_tensor(out=e0[:], in0=e0[:], in1=pP[j], op=ALU.add), [b_e0, b_pP[j]], [b_e0])
                if o == 0:
                    kb.op("dve", lambda e, tt=tt, X=X: e.tensor_tensor(out=U[:, tt, :], in0=e0[:], in1=X[:], op=ALU.mult), [b_e0, b_xt[xi]], [b_Ut[tt]])
                else:
                    kb.op("dve", lambda e, X=X: e.tensor_tensor(out=e1[:], in0=e0[:], in1=X[:], op=ALU.mult), [b_e0, b_xt[xi]], [b_e1])
                    kb.dma("sp", lambda e, tt=tt: e.dma_start(out=y[tt * 128:(tt + 1) * 128, :], in_=e1[:]), [b_e1], [b_ABt[0]], accum=(tt > 0))
    kb.finish([b_ABt[0]])
    kb.emit()
    return cx.nc


def build_RW(ctx_out, dbg_steps=None):
    cx = Ctx(); kb = cx.kb
    T = T_B
    zr = cx.din("zr", [T, 1408]); ident_d = cx.din("ident", [128, 128])
    cw = cx.din("cw", [3, 768]); cb = cx.din("cb", [1, 768])
    w0 = cx.din("w0", [1, 512]); a0 = cx.din("a0", [1, 512])
    w2 = cx.din("w2", [2, 96, 256]); a2 = cx.din("a2", [2, 96, 256]); g2 = cx.din("g2", [256, 256])
    vecs = cx.din("vecs", [6, 256])
    y = cx.dout("y", [T, 256])
    SC = cx.dscr("SC", [T, 2560]); b_SC = [Buf() for _ in range(NT_B)]
    BG = cx.dscr("BG", [T, 512]); b_BG = [Buf() for _ in range(NT_B)]
    VTd = cx.dscr("VTd", [64, 4, T]); b_VTd = [Buf() for _ in range(NT_B)]
    YD = cx.dscr("YD", [2, 64, 4, T]); b_YD = [[Buf() for _ in range(NT_B)] for _ in range(2)]
    ident, b_ident = cx.sb("identf", [128, 128], F32)
    kb.dma("sp", lambda e: e.dma_start(out=ident[:], in_=ident_d), [], [b_ident])
    cwv, b_cwv = cx.sb("cwv", [128, 3, 768], F32); cbv, b_cbv = cx.sb("cbv", [128, 768], F32)
    for i in range(3):
        kb.dma("sp", lambda e, i=i: e.dma_start(out=cwv[:, i, :], in_=cw[i:i + 1, :].partition_broadcast(128)), [], [b_cwv], accum=(i > 0))
    kb.dma("sp", lambda e: e.dma_start(out=cbv[:], in_=cb.partition_broadcast(128)), [], [b_cbv])
    w0v, b_w0v = cx.sb("w0v", [128, 512], F32); a0v, b_a0v = cx.sb("a0v", [128, 512], F32)
    kb.dma("sp", lambda e: e.dma_start(out=w0v[:], in_=w0.partition_broadcast(128)), [], [b_w0v])
    kb.dma("sp", lambda e: e.dma_start(out=a0v[:], in_=a0.partition_broadcast(128)), [], [b_a0v])
    VV, b_VV = cx.sb("VV", [128, 7, 256], F32)
    for i in range(6):
        kb.dma("sp", lambda e, i=i: e.dma_start(out=VV[:, i, :], in_=vecs[i:i + 1, :].partition_broadcast(128)), [], [b_VV], accum=(i > 0))
    kb.op("dve", lambda e: e.tensor_scalar(out=VV[:, 2, :], in0=VV[:, 1, :], scalar1=-1.0, scalar2=1.0, op0=ALU.mult, op1=ALU.add), [b_VV], [b_VV])
    w2s, b_w2s = cx.sb("w2s", [96, 2, 256], F32); a2s, b_a2s = cx.sb("a2s", [96, 2, 256], F32); g2s, b_g2s = cx.sb("g2s", [128, 2, 256], F32)
    kb.dma("sp", lambda e: e.dma_start(out=w2s[:], in_=w2.rearrange("d k n -> k d n")), [], [b_w2s])
    kb.dma("sp", lambda e: e.dma_start(out=a2s[:], in_=a2.rearrange("d k n -> k d n")), [], [b_a2s])
    kb.dma("sp", lambda e: e.dma_start(out=g2s[:], in_=g2.rearrange("(c p) n -> p c n", p=128)), [], [b_g2s])
    Z3, b_Z3 = cx.sb("Z3", [128, 3, 768], F32)
    Zx, b_Zx = cx.sb("Zx", [128, 640], F32)
    C, b_C = cx.sb("C", [128, 768], F32)
    TT, b_TT = cx.sb("TT", [128, 6, 128], F32)
    SCt, b_SCt = cx.sb("SCt", [128, 2, 5, 256], F32)
    BGt, b_BGt = cx.sb("BGt", [128, 2, 256], F32)
    t256, b_t256 = cx.sb("t256", [128, 256], F32, 3)
    s4, b_s4 = cx.sb("s4", [128, 8], F32)
    vTs, b_vTs = cx.sb("vTs", [64, 4, 128], F32)
    pT6, b_pT6 = cx.ps("pT6", [128, 4, 128], F32, 2)
    pL, b_pL = cx.ps("pL", [128, 512], F32, 2)
    pG, b_pG = cx.ps("pG", [128, 512], F32)
    E05 = math.exp(-0.5)

    def seq_bounds(tt):
        first = tt in (0, 32); last = tt in (31, 33)
        return first, last

    for tt in range(NT_B):
        r0 = tt * 128
        first, last = seq_bounds(tt)
        if first or last:
            kb.op("pool", lambda e: e.memset(Z3[:], 0.0), [], [b_Z3])
        dep = [b_Z3] if (first or last) else []
        if first:
            kb.dma("sp", lambda e, r0=r0: e.dma_start(out=Z3[1:128, 0, :], in_=zr[r0:r0 + 127, 0:768]), dep, [b_Z3], accum=True)
        else:
            kb.dma("sp", lambda e, r0=r0: e.dma_start(out=Z3[:, 0, :], in_=zr[r0 - 1:r0 + 127, 0:768]), dep, [b_Z3], accum=bool(dep))
        kb.dma("sp", lambda e, r0=r0: e.dma_start(out=Z3[:, 1, :], in_=zr[r0:r0 + 128, 0:768]), dep, [b_Z3], accum=True)
        if last:
            kb.dma("sp", lambda e, r0=r0: e.dma_start(out=Z3[0:127, 2, :], in_=zr[r0 + 1:r0 + 128, 0:768]), dep, [b_Z3], accum=True)
        else:
            kb.dma("sp", lambda e, r0=r0: e.dma_start(out=Z3[:, 2, :], in_=zr[r0 + 1:r0 + 129, 0:768]), dep, [b_Z3], accum=True)
        kb.dma("sp", lambda e, r0=r0: e.dma_start(out=Zx[:], in_=zr[r0:r0 + 128, 768:1408]), [], [b_Zx])
        kb.op("dve", lambda e: e.tensor_tensor(out=Z3[:], in0=Z3[:], in1=cwv[:], op=ALU.mult), [b_Z3, b_cwv], [b_Z3])
        kb.op("dve", lambda e: e.tensor_tensor(out=C[:], in0=Z3[:, 0, :], in1=Z3[:, 1, :], op=ALU.add), [b_Z3], [b_C])
        kb.op("dve", lambda e: e.tensor_tensor(out=C[:], in0=C[:], in1=Z3[:, 2, :], op=ALU.add), [b_Z3, b_C], [b_C])
        kb.op("dve", lambda e: e.tensor_tensor(out=C[:], in0=C[:], in1=cbv[:], op=ALU.add), [b_C, b_cbv], [b_C])
        R = C[:, 0:256]; Kk = C[:, 256:512]; Vv = C[:, 512:768]
        kb.op("act", lambda e: e.activation(out=Zx[:, 0:192], in_=Zx[:, 0:192], func=AF.Tanh), [b_Zx], [b_Zx])
        kb.op("act", lambda e: e.activation(out=Zx[:, 384:640], in_=Zx[:, 384:640], func=AF.Sigmoid), [b_Zx], [b_Zx])
        srcs = [(0, 96), (96, 96), (192, 96), (288, 96), (384, 128), (512, 128)]
        for q, (c0, wd) in enumerate(srcs):
            p = q // 4
            kb.op("pe", lambda e, p=p, q=q, c0=c0, wd=wd: e.transpose(out=pT6[p][0:wd, q % 4, :], in_=Zx[:, c0:c0 + wd], identity=ident[:]), [b_Zx, b_ident], [b_pT6[p]])
        kb.op("act", lambda e: e.copy(out=TT[:, 0:4, :], in_=pT6[0][:]), [b_pT6[0]], [b_TT])
        kb.op("dve", lambda e: e.tensor_copy(out=TT[:, 4:6, :], in_=pT6[1][:, 0:2, :]), [b_pT6[1]], [b_TT], accum=True)
        for d in range(2):
            kb.op("pe", lambda e, d=d: e.matmul(pL[0][:, d * 256:(d + 1) * 256], lhsT=TT[0:96, d, :], rhs=w2s[:, d, :], start=True, stop=True), [b_TT, b_w2s], [b_pL[0]])
        for d in range(2):
            kb.op("pe", lambda e, d=d: e.matmul(pL[1][:, d * 256:(d + 1) * 256], lhsT=TT[0:96, 2 + d, :], rhs=a2s[:, d, :], start=True, stop=True), [b_TT, b_a2s], [b_pL[1]])
        for c in range(2):
            kb.op("pe", lambda e, c=c: e.matmul(pG[:, 0:256], lhsT=TT[:, 4 + c, :], rhs=g2s[:, c, :], start=(c == 0), stop=(c == 1)), [b_TT, b_g2s], [b_pG])
        S5 = SCt
        wv = S5[:, :, 0, :]; av = S5[:, :, 1, :]; bv = S5[:, :, 2, :]; kv = S5[:, :, 3, :]; rv = S5[:, :, 4, :]
        kb.op("dve", lambda e: e.tensor_tensor(out=wv, in0=pL[0][:].rearrange("p (d c) -> p d c", d=2), in1=w0v[:].rearrange("p (d c) -> p d c", d=2), op=ALU.add), [b_pL[0], b_w0v], [b_SCt])
        kb.op("act", lambda e: e.activation(out=wv, in_=wv, func=AF.Sigmoid), [b_SCt], [b_SCt])
        kb.op("act", lambda e: e.activation(out=wv, in_=wv, func=AF.Exp, scale=-E05), [b_SCt], [b_SCt])
        kb.op("dve", lambda e: e.tensor_tensor(out=bv, in0=pL[1][:].rearrange("p (d c) -> p d c", d=2), in1=a0v[:].rearrange("p (d c) -> p d c", d=2), op=ALU.add), [b_pL[1], b_a0v], [b_SCt])
        kb.op("act", lambda e: e.activation(out=bv, in_=bv, func=AF.Sigmoid), [b_SCt], [b_SCt])
        KKt = t256[0]; TMP = t256[1]; TM2 = t256[2]
        kb.op("dve", lambda e: e.tensor_tensor(out=KKt[:], in0=Kk, in1=VV[:, 0, :], op=ALU.mult), [b_C, b_VV], [b_t256[0]])
        kb.op("dve", lambda e: e.tensor_tensor(out=TMP[:], in0=KKt[:], in1=KKt[:], op=ALU.mult), [b_t256[0]], [b_t256[1]])
        kb.op("dve", lambda e: e.tensor_reduce(out=s4[:, 0:4], in_=TMP[:].rearrange("p (h j) -> p h j", j=64), axis=AX.X, op=ALU.add), [b_t256[1]], [b_s4])
        kb.op("act", lambda e: e.activation(out=s4[:, 0:4], in_=s4[:, 0:4], func=AF.Sqrt, bias=1e-12, scale=1.0), [b_s4], [b_s4])
        kb.op("dve", lambda e: e.reciprocal(out=s4[:, 0:4], in_=s4[:, 0:4]), [b_s4], [b_s4])
        kb.op("dve", lambda e: e.tensor_tensor(out=KKt[:].rearrange("p (h j) -> p h j", j=64), in0=KKt[:].rearrange("p (h j) -> p h j", j=64),
                                               in1=s4[:, 0:4].unsqueeze(2).broadcast_to([128, 4, 64]), op=ALU.mult), [b_t256[0], b_s4], [b_t256[0]])
        for d in range(2):
            kb.op("dve", lambda e, d=d: e.tensor_tensor(out=TMP[:], in0=S5[:, d, 2, :], in1=VV[:, 1, :], op=ALU.mult), [b_SCt, b_VV], [b_t256[1]])
            kb.op("dve", lambda e: e.tensor_tensor(out=TMP[:], in0=TMP[:], in1=VV[:, 2, :], op=ALU.add), [b_t256[1], b_VV], [b_t256[1]])
            kb.op("dve", lambda e, d=d: e.tensor_tensor(out=S5[:, d, 3, :], in0=TMP[:], in1=Kk, op=ALU.mult), [b_t256[1], b_C], [b_SCt])
            kb.op("dve", lambda e, d=d: e.tensor_tensor(out=S5[:, d, 2, :], in0=S5[:, d, 2, :], in1=KKt[:], op=ALU.mult), [b_SCt, b_t256[0]], [b_SCt])
            kb.op("dve", lambda e, d=d: e.tensor_scalar(out=S5[:, d, 1, :], in0=KKt[:], scalar1=-1.0, scalar2=None, op0=ALU.mult), [b_t256[0]], [b_SCt])
            kb.op("act", lambda e, d=d: e.copy(out=S5[:, d, 4, :], in_=R), [b_C], [b_SCt])
        kb.dma("sp", lambda e, r0=r0: e.dma_start(out=SC[r0:r0 + 128, :], in_=SCt[:].rearrange("p d q c -> p (d q c)")), [b_SCt], [b_SC[tt]])
        kb.op("dve", lambda e: e.tensor_tensor(out=TM2[:], in0=S5[:, 0, 3, :], in1=S5[:, 1, 3, :], op=ALU.add), [b_SCt], [b_t256[2]])
        kb.op("dve", lambda e: e.tensor_tensor(out=TM2[:], in0=TM2[:], in1=R, op=ALU.mult), [b_t256[2], b_C], [b_t256[2]])
        kb.op("dve", lambda e: e.tensor_tensor(out=TM2[:], in0=TM2[:], in1=VV[:, 3, :], op=ALU.mult), [b_t256[2], b_VV], [b_t256[2]])
        kb.op("dve", lambda e: e.tensor_reduce(out=s4[:, 4:8], in_=TM2[:].rearrange("p (h j) -> p h j", j=64), axis=AX.X, op=ALU.add), [b_t256[2]], [b_s4])
        kb.op("dve", lambda e: e.tensor_tensor(out=BGt[:, 0, :].rearrange("p (h j) -> p h j", j=64), in0=Vv.rearrange("p (h j) -> p h j", j=64),
                                               in1=s4[:, 4:8].unsqueeze(2).broadcast_to([128, 4, 64]), op=ALU.mult), [b_C, b_s4], [b_BGt])
        kb.op("act", lambda e: e.copy(out=BGt[:, 1, :], in_=pG[:, 0:256]), [b_pG], [b_BGt], accum=True)
        kb.dma("sp", lambda e, r0=r0: e.dma_start(out=BG[r0:r0 + 128, :], in_=BGt[:].rearrange("p a c -> p (a c)")), [b_BGt], [b_BG[tt]])
        for hh in range(4):
            kb.op("pe", lambda e, hh=hh: e.transpose(out=pT6[1][0:64, hh, :], in_=C[:, 512 + hh * 64:512 + (hh + 1) * 64], identity=ident[:]), [b_C, b_ident], [b_pT6[1]])
        kb.op("act", lambda e: e.copy(out=vTs[:], in_=pT6[1][0:64, :, :]), [b_pT6[1]], [b_vTs])
        kb.dma("sp", lambda e, r0=r0: e.dma_start(out=VTd[:, :, r0:r0 + 128], in_=vTs[:]), [b_vTs], [b_VTd[tt]])

    S, b_S = cx.sb("S", [64, 8, 64], F32)
    kb.op("pool", lambda e: e.memset(S[:], 0.0), [], [b_S])
    NBC = 4
    BC, b_BC = cx.sb("BC", [64, 5, 2, 256], F32, NBC)
    vblk, b_vblk = cx.sb("vblk", [64, 2, 4, 128], F32, 2)
    yblk, b_yblk = cx.sb("yblk", [64, 2, 4, 128], F32, 2)
    t1, b_t1 = cx.sb("t1", [64, 8, 64], F32); t2, b_t2 = cx.sb("t2", [64, 8, 64], F32); t3, b_t3 = cx.sb("t3", [64, 8, 64], F32)
    sa, b_sa = cx.sb("sa", [64, 8], F32)

    def bview(Bt, q):
        return Bt[:, q, :, :].rearrange("p d (h j) -> p (d h) j", j=64)

    segs = [(4096, 256), (0, 4096)]
    nsteps = 0
    for (base, ln) in segs:
        nblk = ln // 128
        for blk in range(nblk):
            tf0 = base + blk * 128
            tb0 = base + ln - (blk + 1) * 128
            ttf = tf0 // 128; ttb = tb0 // 128
            vi = cx.nxt("vblk", 2); Vb = vblk[vi]; Yb = yblk[vi]
            kb.dma("act", lambda e, Vb=Vb, tf0=tf0: e.dma_start(out=Vb[:, 0, :, :], in_=VTd[:, :, tf0:tf0 + 128]), [b_VTd[ttf]], [b_vblk[vi]])
            kb.dma("act", lambda e, Vb=Vb, tb0=tb0: e.dma_start(out=Vb[:, 1, :, :], in_=VTd[:, :, tb0:tb0 + 128]), [b_VTd[ttb]], [b_vblk[vi]], accum=True)
            for r in range(128):
                if dbg_steps is not None and nsteps >= dbg_steps:
                    break
                nsteps += 1
                tokf = tf0 + r; tokb = tb0 + 127 - r
                bi = cx.nxt("BC", NBC); Bt = BC[bi]
                q1 = "sp" if (r % 2 == 0) else "act"
                kb.dma("sp", lambda e, Bt=Bt, tokf=tokf: e.dma_start(out=Bt[:, :, 0, :], in_=SC[tokf:tokf + 1, 0:1280].partition_broadcast(64).rearrange("p o (q c) -> p (o q) c", q=5)),
                       [b_SC[tokf // 128]], [b_BC[bi]])
                kb.dma("sp", lambda e, Bt=Bt, tokb=tokb: e.dma_start(out=Bt[:, :, 1, :], in_=SC[tokb:tokb + 1, 1280:2560].partition_broadcast(64).rearrange("p o (q c) -> p (o q) c", q=5)),
                       [b_SC[tokb // 128]], [b_BC[bi]], accum=True)
                W = bview(Bt, 0); A = bview(Bt, 1); Bb = bview(Bt, 2); Kq = bview(Bt, 3); Rr = bview(Bt, 4)
                kb.op("dve", lambda e, A=A: e.tensor_tensor(out=t1[:], in0=S[:], in1=A, op=ALU.mult), [b_S, b_BC[bi]], [b_t1])
                kb.op("dve", lambda e: e.tensor_reduce(out=sa[:], in_=t1[:], axis=AX.X, op=ALU.add), [b_t1], [b_sa])
                kb.op("pool", lambda e, Kq=Kq, Vb=Vb, r=r: e.tensor_tensor(out=t3[:, 0:4, :], in0=Kq[:, 0:4, :], in1=Vb[:, 0, :, r:r + 1].broadcast_to([64, 4, 64]), op=ALU.mult),
                      [b_BC[bi], b_vblk[vi]], [b_t3])
                kb.op("pool", lambda e, Kq=Kq, Vb=Vb, r=r: e.tensor_tensor(out=t3[:, 4:8, :], in0=Kq[:, 4:8, :], in1=Vb[:, 1, :, 127 - r:128 - r].broadcast_to([64, 4, 64]), op=ALU.mult),
                      [b_BC[bi], b_vblk[vi]], [b_t3], accum=True)
                kb.op("dve", lambda e, W=W: e.tensor_tensor(out=S[:], in0=S[:], in1=W, op=ALU.mult), [b_S, b_BC[bi]], [b_S])
                kb.op("dve", lambda e, Bb=Bb: e.tensor_tensor(out=t2[:], in0=Bb, in1=sa[:].unsqueeze(2).broadcast_to([64, 8, 64]), op=ALU.mult), [b_BC[bi], b_sa], [b_t2])
                kb.op("dve", lambda e: e.tensor_tensor(out=S[:], in0=S[:], in1=t2[:], op=ALU.add), [b_S, b_t2], [b_S])
                kb.op("dve", lambda e: e.tensor_tensor(out=S[:], in0=S[:], in1=t3[:], op=ALU.add), [b_S, b_t3], [b_S])
                kb.op("dve", lambda e, Rr=Rr: e.tensor_tensor(out=t1[:], in0=S[:], in1=Rr, op=ALU.mult), [b_S, b_BC[bi]], [b_t1])
                kb.op("dve", lambda e, Yb=Yb, r=r: e.tensor_reduce(out=Yb[:, 0, :, r:r + 1], in_=t1[:, 0:4, :], axis=AX.X, op=ALU.add), [b_t1], [b_yblk[vi]], accum=True)
                kb.op("dve", lambda e, Yb=Yb, r=r: e.tensor_reduce(out=Yb[:, 1, :, 127 - r:128 - r], in_=t1[:, 4:8, :], axis=AX.X, op=ALU.add), [b_t1], [b_yblk[vi]], accum=True)
            kb.dma("act", lambda e, Yb=Yb, tf0=tf0: e.dma_start(out=YD[0, :, :, tf0:tf0 + 128], in_=Yb[:, 0, :, :]), [b_yblk[vi]], [b_YD[0][ttf]])
            kb.dma("act", lambda e, Yb=Yb, tb0=tb0: e.dma_start(out=YD[1, :, :, tb0:tb0 + 128], in_=Yb[:, 1, :, :]), [b_yblk[vi]], [b_YD[1][ttb]])
            kb.op("pool", lambda e, Yb=Yb: e.memset(Yb[:, :, :, 0:1], 0.0), [], [b_yblk[vi]])

    yf, b_yf = cx.sb("yf", [64, 2, 4, 128], F32)
    Yt, b_Yt = cx.sb("Yt", [128, 256], F32)
    bg, b_bg = cx.sb("bg", [128, 512], F32)
    st6, b_st6 = cx.sb("st6", [128, 4, 6], F32); mvs, b_mvs = cx.sb("mvs", [128, 4, 2], F32); rs4, b_rs4 = cx.sb("rs4", [128, 4], F32)
    b_out = Buf()
    tiles = list(range(NT_B)) if ctx_out else list(range(32))
    for n, tt in enumerate(tiles):
        r0 = tt * 128
        kb.dma("sp", lambda e, r0=r0: e.dma_start(out=yf[:, 0, :, :], in_=YD[0, :, :, r0:r0 + 128]), [b_YD[0][tt]], [b_yf])
        kb.dma("sp", lambda e, r0=r0: e.dma_start(out=yf[:, 1, :, :], in_=YD[1, :, :, r0:r0 + 128]), [b_YD[1][tt]], [b_yf], accum=True)
        kb.dma("sp", lambda e, r0=r0: e.dma_start(out=bg[:], in_=BG[r0:r0 + 128, :]), [b_BG[tt]], [b_bg])
        kb.op("dve", lambda e: e.tensor_tensor(out=yf[:, 0, :, :], in0=yf[:, 0, :, :], in1=yf[:, 1, :, :], op=ALU.add), [b_yf], [b_yf])
        for hh in range(4):
            kb.op("pe", lambda e, hh=hh: e.transpose(out=pL[0][:, hh * 64:(hh + 1) * 64], in_=yf[:, 0, hh, :], identity=ident[0:64, 0:64]), [b_yf, b_ident], [b_pL[0]])
        kb.op("act", lambda e: e.copy(out=Yt[:], in_=pL[0][:, 0:256]), [b_pL[0]], [b_Yt])
        for hh in range(4):
            kb.op("dve", lambda e, hh=hh: e.bn_stats(out=st6[:, hh, :], in_=Yt[:, hh * 64:(hh + 1) * 64]), [b_Yt], [b_st6])
        for hh in range(4):
            kb.op("dve", lambda e, hh=hh: e.bn_aggr(out=mvs[:, hh, :], in_=st6[:, hh, :]), [b_st6], [b_mvs])
        kb.op("act", lambda e: e.activation(out=rs4[:], in_=mvs[:, :, 1], func=AF.Sqrt, bias=64e-5, scale=1.0), [b_mvs], [b_rs4])
        kb.op("dve", lambda e: e.reciprocal(out=rs4[:], in_=rs4[:]), [b_rs4], [b_rs4])
        Y3 = Yt[:].rearrange("p (h j) -> p h j", j=64)
        kb.op("dve", lambda e, Y3=Y3: e.tensor_tensor(out=Y3, in0=Y3, in1=mvs[:, :, 0:1].broadcast_to([128, 4, 64]), op=ALU.subtract), [b_Yt, b_mvs], [b_Yt])
        kb.op("dve", lambda e, Y3=Y3: e.tensor_tensor(out=Y3, in0=Y3, in1=rs4[:].unsqueeze(2).broadcast_to([128, 4, 64]), op=ALU.mult), [b_Yt, b_rs4], [b_Yt])
        kb.op("dve", lambda e: e.tensor_tensor(out=Yt[:], in0=Yt[:], in1=VV[:, 4, :], op=ALU.mult), [b_Yt, b_VV], [b_Yt])
        kb.op("dve", lambda e: e.tensor_tensor(out=Yt[:], in0=Yt[:], in1=VV[:, 5, :], op=ALU.add), [b_Yt, b_VV], [b_Yt])
        kb.op("dve", lambda e: e.tensor_tensor(out=Yt[:], in0=Yt[:], in1=bg[:, 0:256], op=ALU.add), [b_Yt, b_bg], [b_Yt])
        kb.op("dve", lambda e: e.tensor_tensor(out=Yt[:], in0=Yt[:], in1=bg[:, 256:512], op=ALU.mult), [b_Yt, b_bg], [b_Yt])
        kb.dma("sp", lambda e, r0=r0: e.dma_start(out=y[r0:r0 + 128, :], in_=Yt[:]), [b_Yt], [b_out], accum=(n > 0))
    kb.finish([b_out])
    kb.emit()
    return cx.nc


DEPTH = 4
IDENT = np.eye(128, dtype=np.float32)
_NC_CACHE = {}


def get_nc(key, fn):
    if key not in _NC_CACHE:
        _NC_CACHE[key] = fn()
    return _NC_CACHE[key]


def shard_rows(xl, xcx):
    out = []
    for c in range(8):
        b, h = c // 2, c % 2
        out.append(np.ascontiguousarray(np.concatenate([xl[b, h * 2048:(h + 1) * 2048], xcx[b, h * 128:(h + 1) * 128]], axis=0)))
    return out


def unshard_rows(lst):
    C = lst[0].shape[1]
    xl = np.empty((4, 4096, C), lst[0].dtype); xcx = np.empty((4, 256, C), lst[0].dtype)
    for c in range(8):
        b, h = c // 2, c % 2
        xl[b, h * 2048:(h + 1) * 2048] = lst[c][:2048]
        xcx[b, h * 128:(h + 1) * 128] = lst[c][2048:]
    return xl, xcx


def run_A(l, inp, xl, xcx, ycat=None, yccat=None, modprev=None, cores=range(8)):
    has_prev = l > 0
    has_cur = l < DEPTH
    nc = get_nc(("A", has_prev, has_cur), lambda: build_A(has_prev, has_cur))
    xs = shard_rows(xl, xcx)
    if has_prev:
        ys = shard_rows(ycat, yccat)
    maps = []
    for c in cores:
        b = c // 2
        m = {"xin": xs[c], "ident": IDENT}
        if has_prev:
            p = l - 1
            m.update(ycat=ys[c], modprev=modprev[c], wout=inp["w_out"][p], pw1=inp["ffn_w1"][p, 1], pw3=inp["ffn_w3"][p, 1],
                     pw2=inp["ffn_w2"][p, 1], plng=inp["ln_g"][p], plnb=inp["ln_b"][p])
        if has_cur:
            m.update(cvec=np.ascontiguousarray(np.stack([inp["c"][b], inp["c_ctx"]])), adaw=inp["ada_w"][l], adab=inp["ada_b"][l][None, :],
                     w1=inp["ffn_w1"][l, 0], w3=inp["ffn_w3"][l, 0], w2=inp["ffn_w2"][l, 0], win=inp["w_in"][l],
                     lng=inp["ln_g"][l], lnb=inp["ln_b"][l])
        maps.append(m)
    res = run_bass_kernel_spmd(nc, maps, core_ids=list(range(len(maps))))
    return res.results


def rope_tables(rot_dim, scale=1.0):
    rows = 4096 // 64
    row = np.repeat(np.arange(rows), 64); col = np.tile(np.arange(64), rows)
    nf = rot_dim // 4
    inv = (10000.0 ** (-np.arange(nf, dtype=np.float32) / nf)).astype(np.float32)
    ang = np.concatenate([row[:, None] * inv, col[:, None] * inv], axis=-1).astype(np.float32)
    cs = np.concatenate([np.cos(ang), np.sin(ang)], axis=-1).astype(np.float32)
    cc = np.concatenate([np.ones((256, rot_dim // 2), np.float32), np.zeros((256, rot_dim // 2), np.float32)], axis=-1)
    return np.ascontiguousarray(np.concatenate([cs, cc], axis=0) * np.float32(scale)).astype(np.float32)


CQ_MLA = rope_tables(32, 96.0 ** -0.5)
CK_MLA = rope_tables(32, 1.0)


def run_MLA(l, inp, zfull, ctx_out, cores=range(8)):
    nc = get_nc(("MLA", ctx_out), lambda: build_MLA(ctx_out))
    o = 1536 + 2176
    maps = []
    for c in cores:
        b, hh = c // 2, c % 2
        maps.append({"zm": np.ascontiguousarray(zfull[b][:, o:o + 544]), "ident": IDENT,
                     "wuq": np.ascontiguousarray(inp["mla_wuq"][l][:, hh * 384:(hh + 1) * 384]),
                     "wukv": np.ascontiguousarray(inp["mla_wukv"][l][:, hh * 512:(hh + 1) * 512]),
                     "qg": inp["mla_q_g"][l][None, :], "kvg": inp["mla_kv_g"][l][None, :], "cq": CQ_MLA, "ck": CK_MLA})
    res = run_bass_kernel_spmd(nc, maps, core_ids=list(range(len(maps))))
    return res.results


def _dft_tables(L):
    NP = (L // 128 + 1) * 128
    r = np.arange(NP, dtype=np.int64)
    prod = (r[:, None] * r[None, :]) % (2 * L)
    ang = prod.astype(np.float64) * (np.pi / L)
    valid = ((r[:, None] <= L) & (r[None, :] <= L))
    Cc = np.where(valid, np.cos(ang), 0.0).astype(np.float32)
    Ss = np.where(valid, np.sin(ang), 0.0).astype(np.float32)
    k = np.arange(NP)
    w = np.where((k == 0) | (k == L), 1.0, np.where(k < L, 2.0, 0.0)) / (2.0 * L)
    wf = np.ascontiguousarray(w.reshape(NP // 128, 128).T).astype(np.float32)
    return Cc, Ss, wf


def _hy_consts(L):
    t = np.linspace(0.0, 1.0, L, dtype=np.float32)[:, None]
    w = (2.0 * np.pi * np.arange(L, dtype=np.float32)[:, None] / L).astype(np.float32)
    f = np.linspace(1e-4, 15, 16, dtype=np.float32)[None, :]
    feats = np.concatenate([t, np.cos(f * w), -np.sin(f * w)], axis=-1).astype(np.float32)
    deltas = np.abs(np.linspace(np.log(1e-2) / 0.3, np.log(1e-2) / 1.5, 512, dtype=np.float32))
    dw = np.exp(-t * deltas[None, :]).astype(np.float32)
    return np.ascontiguousarray(feats.T), dw


_HYC = {}


def hy_consts(L):
    if L not in _HYC:
        _HYC[L] = _dft_tables(L) + _hy_consts(L)
    return _HYC[L]


def run_HY(l, inp, zfull, L, cores=range(8)):
    nc = get_nc(("HY", L), lambda: build_HY(L))
    Cc, Ss, wf, featsT, dw = hy_consts(L)
    maps = []
    for c in cores:
        b, ch = c // 2, c % 2
        cols = np.concatenate([np.arange(ch * 256, (ch + 1) * 256) + o for o in (0, 512, 1024)])
        zz = zfull[b][:4096] if L == 4096 else zfull[b][4096:]
        dwf = np.ascontiguousarray(dw[:, ch * 256:(ch + 1) * 256]); dwb = dwf.copy(); dwb[0] = 0.0
        w3 = inp["hy_f_w3"][l]
        w3c = np.concatenate([w3[:, (o * 2 + d) * 512 + ch * 256:(o * 2 + d) * 512 + (ch + 1) * 256] for o in range(2) for d in range(2)], axis=1)
        pv = np.stack([inp["hy_f_b1"][l], inp["hy_f_b2"][l], inp["hy_f_freq"][l][0], inp["hy_f_freq"][l][1]], axis=1)
        hb = np.concatenate([inp["hy_bias"][l][o, ch * 256:(ch + 1) * 256] for o in range(2)])[None, :]
        maps.append({"zh": np.ascontiguousarray(zz[:, cols]), "cw": np.ascontiguousarray(inp["hy_conv_w"][l][:, cols]),
                     "cb": np.ascontiguousarray(inp["hy_conv_b"][l][cols][None, :]), "featsT": featsT, "w1": inp["hy_f_w1"][l], "w2": inp["hy_f_w2"][l],
                     "pv": np.ascontiguousarray(pv), "w3c": np.ascontiguousarray(w3c), "dwf": dwf, "dwb": dwb, "hb": np.ascontiguousarray(hb),
                     "wf": wf, "Cc": Cc, "Ss": Ss})
    res = run_bass_kernel_spmd(nc, maps, core_ids=list(range(len(maps))))
    return res.results


def kernel(**inp):
    inp = {k: np.ascontiguousarray(np.asarray(v, dtype=np.float32)) for k, v in inp.items()}
    xl, xcx = inp["x"], inp["ctx"]
    ycat = yccat = modprev = None
    for l in range(DEPTH + 1):
        res = run_A(l, inp, xl, xcx, ycat, yccat, modprev)
        if l == DEPTH:
            xl, _ = unshard_rows([r["xfin"] for r in res])
            break
        xl, xcx = unshard_rows([r["x1"] for r in res])
        zl, zc = unshard_rows([r["z"] for r in res])
        modprev = [r["modout"] for r in res]
        zfull = np.concatenate([zl, zc], axis=1)
        del zl, zc, res
        ctx_out = l < DEPTH - 1
        ycat = np.zeros((4, 4096, 2048), np.float32); yccat = np.zeros((4, 256, 2048), np.float32)

        def put(r, off, key="y"):
            for c in range(8):
                b, hh = c // 2, c % 2
                ycat[b, :, off + hh * 256:off + (hh + 1) * 256] = r[c][key][:4096]
                if ctx_out:
                    yccat[b, :, off + hh * 256:off + (hh + 1) * 256] = r[c][key][4096:]
        r = run_HY(l, inp, zfull, 4096)
        for c in range(8):
            b, hh = c // 2, c % 2
            ycat[b, :, hh * 256:(hh + 1) * 256] = r[c]["y"]
        if ctx_out:
            r = run_HY(l, inp, zfull, 256)
            for c in range(8):
                b, hh = c // 2, c % 2
                yccat[b, :, hh * 256:(hh + 1) * 256] = r[c]["y"]
        put(run_RW(l, inp, zfull, True), 512)
        put(run_MLA(l, inp, zfull, True), 1024)
        put(run_SWA(l, inp, zfull, True), 1536)
    return xl.astype(np.float32)


MIXERS = ()

CQ_SWA = rope_tables(64, 0.125)
CK_SWA = rope_tables(64, 1.0)
_ki = np.arange(128)[:, None]; _qi = np.arange(128)[None, :]
M_PREV = (_ki >= _qi).astype(np.float32)
M_NEXT = (_ki <= _qi).astype(np.float32)


def run_SWA(l, inp, zfull, ctx_out, cores=range(8)):
    nc = get_nc(("SWA", ctx_out), lambda: build_SWA(ctx_out))
    o = 1536 + 2176 + 544
    maps = []
    for c in cores:
        b, hh = c // 2, c % 2
        zz = zfull[b]
        zs = np.concatenate([zz[:, o + hh * 256:o + (hh + 1) * 256], zz[:, o + 512 + hh * 64:o + 512 + (hh + 1) * 64],
                             zz[:, o + 640 + hh * 64:o + 640 + (hh + 1) * 64]], axis=1)
        maps.append({"zs": np.ascontiguousarray(zs), "ident": IDENT, "cq": CQ_SWA, "ck": CK_SWA, "mprev": M_PREV, "mnext": M_NEXT,
                     "sink": np.ascontiguousarray(inp["swa_sink"][l][None, hh * 4:(hh + 1) * 4])})
    res = run_bass_kernel_spmd(nc, maps, core_ids=list(range(len(maps))))
    return res.results


def run_RW(l, inp, zfull, ctx_out, cores=range(8), dbg_steps=None):
    nc = get_nc(("RW", ctx_out, dbg_steps), lambda: build_RW(ctx_out, dbg_steps))
    o = 1536
    maps = []
    for c in cores:
        b, hh = c // 2, c % 2
        my = np.arange(hh * 256, (hh + 1) * 256)
        cols = np.concatenate([my, 512 + my, 1024 + my])
        zz = zfull[b]
        zr = np.concatenate([zz[:, o + cols], zz[:, o + 1536:o + 2176]], axis=1)
        vecs = np.stack([inp["rw_k_k"][l][my], inp["rw_k_a"][l][my], np.zeros(256, np.float32), inp["rw_r_k"][l].reshape(-1)[my],
                         inp["rw_gn_g"][l][my], inp["rw_gn_b"][l][my]]).astype(np.float32)
        maps.append({"zr": np.ascontiguousarray(zr), "ident": IDENT, "cw": np.ascontiguousarray(inp["rw_conv_w"][l][:, cols]),
                     "cb": np.ascontiguousarray(inp["rw_conv_b"][l][cols][None, :]),
                     "w0": np.ascontiguousarray(inp["rw_w0"][l][:, my].reshape(1, 512)), "a0": np.ascontiguousarray(inp["rw_a0"][l][:, my].reshape(1, 512)),
                     "w2": np.ascontiguousarray(inp["rw_w2"][l][:, :, my]), "a2": np.ascontiguousarray(inp["rw_a2"][l][:, :, my]),
                     "g2": np.ascontiguousarray(inp["rw_g2"][l][:, my]), "vecs": np.ascontiguousarray(vecs)})
    res = run_bass_kernel_spmd(nc, maps, core_ids=list(range(len(maps))))
    return res.results
```
